# Optimizing a Trainium2 kernel written in Bass

```python
import math
import jax, jax.numpy as jnp
from jax import lax
import numpy as np

D_MODEL = 1024
BATCH = 4
SEQ = 8192
DEPTH = 4

HEAD_DIM = 64
BLOCK = 128
SWA_Q_HEADS = 8
SWA_KV_HEADS = 2
SWA_WINDOW = 128
DIFF_HEADS = 4
DIFF_V_DIM = 2 * HEAD_DIM
BRANCH_WIDTH = SWA_Q_HEADS * HEAD_DIM
N_BRANCHES = 2
N_ALIBI_HEADS = SWA_Q_HEADS + DIFF_HEADS
FFN_HIDDEN = -(-8 * D_MODEL // (3 * 256)) * 256
COL_SWA_Q = SWA_Q_HEADS * HEAD_DIM
COL_SWA_K = SWA_KV_HEADS * HEAD_DIM
COL_SWA_V = SWA_KV_HEADS * HEAD_DIM
COL_DIFF_Q = DIFF_HEADS * 2 * HEAD_DIM
COL_DIFF_K = DIFF_HEADS * 2 * HEAD_DIM
COL_DIFF_V = DIFF_HEADS * DIFF_V_DIM
COL_GATE = N_BRANCHES * D_MODEL
IN_COLS = COL_SWA_Q + COL_SWA_K + COL_SWA_V + COL_DIFF_Q + COL_DIFF_K + COL_DIFF_V + COL_GATE
NEG = -1e30
EPS = 1e-6

kernel_name = "hybrid_swa_sink_diffattn_gated"


def rms_norm(x, gain):
    xf = x.astype(jnp.float32)
    y = xf * lax.rsqrt(jnp.mean(xf * xf, axis=-1, keepdims=True) + EPS)
    return (y * gain.astype(jnp.float32)).astype(x.dtype)


def alibi_slopes():
    return jnp.exp2(-8.0 * jnp.arange(1, N_ALIBI_HEADS + 1, dtype=jnp.float32) / N_ALIBI_HEADS)


def swa_attention(q, k, v, sinks, slopes):
    bsz, seq = q.shape[0], q.shape[1]
    nb = seq // BLOCK
    grp = SWA_Q_HEADS // SWA_KV_HEADS
    qb = q.reshape(bsz, nb, BLOCK, SWA_KV_HEADS, grp, HEAD_DIM)
    kb = k.reshape(bsz, nb, BLOCK, SWA_KV_HEADS, HEAD_DIM)
    vb = v.reshape(bsz, nb, BLOCK, SWA_KV_HEADS, HEAD_DIM)
    pad = ((0, 0), (1, 0), (0, 0), (0, 0), (0, 0))
    kcat = jnp.concatenate([jnp.pad(kb, pad)[:, :-1], kb], axis=2)
    vcat = jnp.concatenate([jnp.pad(vb, pad)[:, :-1], vb], axis=2)
    s = jnp.einsum("bnqhgd,bnkhd->bnhgqk", qb, kcat).astype(jnp.float32) * (HEAD_DIM ** -0.5)
    dist = (BLOCK + jnp.arange(BLOCK))[:, None] - jnp.arange(2 * BLOCK)[None, :]
    in_window = (dist >= 0) & (dist < SWA_WINDOW)
    has_prev = (jnp.arange(nb)[:, None] > 0) | (jnp.arange(2 * BLOCK)[None, :] >= BLOCK)
    mask = in_window[None] & has_prev[:, None, :]
    bias = -slopes.reshape(SWA_KV_HEADS, grp)[:, :, None, None] * dist.astype(jnp.float32)
    s = jnp.where(mask[None, :, None, None], s + bias, NEG)
    sink = jnp.broadcast_to(sinks.astype(jnp.float32).reshape(SWA_KV_HEADS, grp)[None, None, :, :, None, None],
                            s.shape[:-1] + (1,))
    p = jax.nn.softmax(jnp.concatenate([s, sink], axis=-1), axis=-1)[..., :-1]
    o = jnp.einsum("bnhgqk,bnkhd->bnqhgd", p.astype(v.dtype), vcat)
    return o.reshape(bsz, seq, SWA_Q_HEADS * HEAD_DIM)


def diff_attention(q, k, v, lam, slopes):
    bsz, seq = q.shape[0], q.shape[1]
    nb = seq // BLOCK
    qb = q.reshape(bsz, nb, BLOCK, DIFF_HEADS, 2, HEAD_DIM).transpose(1, 0, 2, 3, 4, 5)
    kpos = jnp.arange(seq)

    def one_block(args):
        q_blk, n = args
        s = jnp.einsum("bqhcd,bkhcd->bhcqk", q_blk, k).astype(jnp.float32) * (HEAD_DIM ** -0.5)
        dist = (n * BLOCK + jnp.arange(BLOCK))[:, None] - kpos[None, :]
        s = s - slopes[None, :, None, None, None] * dist.astype(jnp.float32)
        s = jnp.where(dist >= 0, s, NEG)
        p = jax.nn.softmax(s, axis=-1)
        w = p[:, :, 0] - lam * p[:, :, 1]
        return jnp.einsum("bhqk,bkhe->bqhe", w.astype(v.dtype), v)

    out = lax.map(one_block, (qb, jnp.arange(nb)))
    return out.transpose(1, 0, 2, 3, 4).reshape(bsz, seq, DIFF_HEADS, DIFF_V_DIM)


def setup_inputs(seed: int = 0) -> dict:
    key = jax.random.key(seed)
    ks = jax.random.split(key, 14)
    f32 = jnp.float32
    nrm = lambda k, shape, scale: jax.random.normal(k, shape, f32) * scale
    return {
        "x": jax.random.normal(ks[0], (BATCH, SEQ, D_MODEL), f32),
        "w_in": nrm(ks[1], (DEPTH, D_MODEL, IN_COLS), D_MODEL ** -0.5),
        "b_gate": nrm(ks[2], (DEPTH, N_BRANCHES, D_MODEL), 0.1),
        "w_branch": nrm(ks[3], (DEPTH, N_BRANCHES, BRANCH_WIDTH, D_MODEL), BRANCH_WIDTH ** -0.5),
        "w_o": nrm(ks[4], (DEPTH, D_MODEL, D_MODEL), D_MODEL ** -0.5),
        "norm_mix": 1.0 + nrm(ks[5], (DEPTH, D_MODEL), 0.02),
        "norm_ffn": 1.0 + nrm(ks[6], (DEPTH, D_MODEL), 0.02),
        "qk_norm_swa": 1.0 + nrm(ks[7], (DEPTH, 2, HEAD_DIM), 0.02),
        "qk_norm_diff": 1.0 + nrm(ks[8], (DEPTH, 2, HEAD_DIM), 0.02),
        "attn_sinks": nrm(ks[9], (DEPTH, SWA_Q_HEADS), 0.5),
        "diff_lambda": nrm(ks[10], (DEPTH, 4, HEAD_DIM), 0.1),
        "diff_subln": 1.0 + nrm(ks[11], (DEPTH, DIFF_V_DIM), 0.02),
        "w_ffn_in": nrm(ks[12], (DEPTH, D_MODEL, 2 * FFN_HIDDEN), D_MODEL ** -0.5),
        "w_ffn_out": nrm(ks[13], (DEPTH, FFN_HIDDEN, D_MODEL), FFN_HIDDEN ** -0.5),
    }


def reference(x, w_in, b_gate, w_branch, w_o, norm_mix, norm_ffn, qk_norm_swa, qk_norm_diff,
              attn_sinks, diff_lambda, diff_subln, w_ffn_in, w_ffn_out):
    bsz, seq = x.shape[0], x.shape[1]
    slopes = alibi_slopes()
    slopes_swa, slopes_diff = slopes[:SWA_Q_HEADS], slopes[SWA_Q_HEADS:]
    splits = list(np.cumsum([COL_SWA_Q, COL_SWA_K, COL_SWA_V, COL_DIFF_Q, COL_DIFF_K, COL_DIFF_V]))
    for l in range(DEPTH):
        h = rms_norm(x, norm_mix[l])
        proj = h @ w_in[l]
        qa, ka, va, qd, kd, vd, gate_logits = jnp.split(proj, splits, axis=-1)
        qa = rms_norm(qa.reshape(bsz, seq, SWA_Q_HEADS, HEAD_DIM), qk_norm_swa[l, 0])
        ka = rms_norm(ka.reshape(bsz, seq, SWA_KV_HEADS, HEAD_DIM), qk_norm_swa[l, 1])
        va = va.reshape(bsz, seq, SWA_KV_HEADS, HEAD_DIM)
        o_a = swa_attention(qa, ka, va, attn_sinks[l], slopes_swa)
        qd = rms_norm(qd.reshape(bsz, seq, DIFF_HEADS, 2, HEAD_DIM), qk_norm_diff[l, 0])
        kd = rms_norm(kd.reshape(bsz, seq, DIFF_HEADS, 2, HEAD_DIM), qk_norm_diff[l, 1])
        vd = vd.reshape(bsz, seq, DIFF_HEADS, DIFF_V_DIM)
        lam_init = 0.8 - 0.6 * math.exp(-0.3 * l)
        lp = diff_lambda[l].astype(jnp.float32)
        lam = jnp.exp(jnp.sum(lp[0] * lp[1])) - jnp.exp(jnp.sum(lp[2] * lp[3])) + lam_init
        o_d = diff_attention(qd, kd, vd, lam, slopes_diff)
        o_b = (rms_norm(o_d, diff_subln[l]) * (1.0 - lam_init)).reshape(bsz, seq, BRANCH_WIDTH)
        gates = jax.nn.sigmoid(gate_logits.reshape(bsz, seq, N_BRANCHES, D_MODEL) + b_gate[l])
        branches = jnp.stack([o_a, o_b], axis=2)
        up = jnp.einsum("bsnc,ncd->bsnd", branches, w_branch[l])
        merged = jnp.einsum("bsnd,bsnd->bsd", gates, up)
        x = x + merged @ w_o[l]
        h2 = rms_norm(x, norm_ffn[l])
        g, u = jnp.split(h2 @ w_ffn_in[l], 2, axis=-1)
        x = x + (jax.nn.silu(g) * u) @ w_ffn_out[l]
    return x
```

```python
import math
from contextlib import ExitStack
import numpy as np
import concourse.bass as bass
import concourse.mybir as mybir
from concourse.bass_utils import run_bass_kernel_spmd

F32 = mybir.dt.float32
BF16 = mybir.dt.bfloat16
AF = mybir.ActivationFunctionType
ALU = mybir.AluOpType

D = 1024
NCH = 8
HID = 2816
IN_COLS = 4352
EPS = 1e-6
NEGM = -30000.0
SLOPES = [2.0 ** (-8.0 * i / 12.0) for i in range(1, 13)]

PC_BG = 0
PC_NM = 64
PC_NF = 96
PC_QS = 128
PC_QD = 136
PC_SK = 144
PC_SL = 176
PC_LM = 180
NPAR = 180 + 1024


class Prog:
    COMPUTE = ("pe", "act", "dve", "pool")

    def __init__(self, nc, stack):
        self.nc = nc
        self.stack = stack
        self.q = {k: [] for k in ("pe", "act", "dve", "pool", "sp")}
        self.sem = {}
        self.cnt = {}
        for e in self.COMPUTE:
            self.sem[e] = stack.enter_context(nc.semaphore("prog_" + e))
            self.cnt[e] = 0
        self.dsem = {}
        self.dcnt = {}
        self.last_write = {}
        self.readers = {}
        self.seen = {k: {} for k in self.q}
        self.n_ins = 0

    def _deps(self, reads, writes):
        deps = []
        for r in reads:
            t = self.last_write.get(r)
            if t is not None:
                deps.append(t)
        for w in writes:
            t = self.last_write.get(w)
            if t is not None:
                deps.append(t)
            deps.extend(self.readers.get(w, ()))
        return deps

    def _emit_waits(self, eng, deps):
        seen = self.seen[eng]
        need = {}
        for (owner, sem, val) in deps:
            if owner == "pe" and eng == "pe":
                continue
            key = id(sem)
            if seen.get(key, 0) >= val:
                continue
            if key not in need or need[key][1] < val:
                need[key] = (sem, val)
        for key, (sem, val) in need.items():
            seen[key] = val
            self.q[eng].append(("wait", sem, val))

    def _commit(self, token, reads, writes):
        for w in writes:
            self.last_write[w] = token
            self.readers[w] = []
        for r in reads:
            self.readers.setdefault(r, []).append(token)

    def op(self, eng, fn, reads=(), writes=()):
        self._emit_waits(eng, self._deps(reads, writes))
        self.cnt[eng] += 1
        token = (eng, self.sem[eng], self.cnt[eng])
        fns = fn if isinstance(fn, (list, tuple)) else [fn]
        for f in fns[:-1]:
            self.q[eng].append(("ins", f, None))
        self.q[eng].append(("ins", fns[-1], self.sem[eng]))
        self.n_ins += len(fns)
        self._commit(token, reads, writes)

    def dma(self, queue, fn, semkey, reads=(), writes=()):
        self._emit_waits(queue, self._deps(reads, writes))
        if semkey not in self.dsem:
            self.dsem[semkey] = self.stack.enter_context(self.nc.semaphore("d_" + str(semkey)))
            self.dcnt[semkey] = 0
        self.dcnt[semkey] += 16
        sem = self.dsem[semkey]
        token = ("dma", sem, self.dcnt[semkey])
        self.q[queue].append(("dma", fn, sem))
        self.n_ins += 1
        self._commit(token, reads, writes)

    def barrier(self):
        toks = [(e, self.sem[e], self.cnt[e]) for e in self.COMPUTE if self.cnt[e] > 0]
        toks += [("dma", self.dsem[k], self.dcnt[k]) for k in self.dsem]
        for eng in self.q:
            self._emit_waits(eng, [t for t in toks if not (t[0] == eng and eng != "pe" and False)])
        self.last_write = {}
        self.readers = {}

    def emit(self):
        nc = self.nc
        q = self.q

        def run(engine, items):
            for it in items:
                if it[0] == "wait":
                    engine.wait_ge(it[1], it[2])
                elif it[0] == "ins":
                    ins = it[1](engine)
                    if it[2] is not None:
                        ins.then_inc(it[2], 1)
                else:
                    it[1](engine).then_inc(it[2], 16)

        with nc.Block() as block:
            @block.tensor
            def _(e):
                run(e, q["pe"])

            @block.scalar
            def _(e):
                run(e, q["act"])

            @block.vector
            def _(e):
                run(e, q["dve"])

            @block.gpsimd
            def _(e):
                run(e, q["pool"])

            @block.sync
            def _(e):
                run(e, q["sp"])
        self.q = {k: [] for k in q}


class Ring:
    def __init__(self, name, n):
        self.name, self.n, self.i = name, n, -1

    def next(self):
        self.i = (self.i + 1) % self.n
        return self.i

    def key(self, i):
        return "%s%d" % (self.name, i)


def build_nc(S, depth, lam_inits):
    NG = S // 512
    NB = S // 128
    NG2 = S // 256
    nc = bass.Bass("TRN2", target_bir_lowering=False)
    dt = lambda name, shape, dtype, kind: nc.dram_tensor(name, shape, dtype, kind=kind).ap()
    xT_in = dt("xT", [D, S], F32, "ExternalInput")
    par_in = dt("par", [128, NPAR], F32, "ExternalInput")
    w_in = dt("w_in", [depth, D, IN_COLS], F32, "ExternalInput")
    w_br = dt("w_branch", [depth, 2, 512, D], F32, "ExternalInput")
    w_o = dt("w_o", [depth, D, D], F32, "ExternalInput")
    w_fi = dt("w_ffn_in", [depth, D, 2 * HID], F32, "ExternalInput")
    w_fo = dt("w_ffn_out", [depth, HID, D], F32, "ExternalInput")
    yT = dt("yT", [D, S], F32, "ExternalOutput")
    xs = dt("xs", [D, S], F32, "Internal")
    QaT = dt("QaT", [512, S], BF16, "Internal")
    KaT = dt("KaT", [256, S], BF16, "Internal")
    QdT = dt("QdT", [512, S], BF16, "Internal")
    KdT = dt("KdT", [512, S], BF16, "Internal")
    OaT = dt("OaT", [512, S], BF16, "Internal")
    ObT = dt("ObT", [512, S], BF16, "Internal")
    H2T = dt("H2T", [D, S], BF16, "Internal")
    Va = dt("Va", [128, NB, 130], BF16, "Internal")
    Vd = dt("Vd", [4, 128, NB, 129], BF16, "Internal")

    chunked = lambda ap2d: ap2d.rearrange("(c p) t -> p c t", p=128)

    with ExitStack() as st:
        P = Prog(nc, st)
        uid = [0]

        def sbuf(stk, name, shape, dtype):
            uid[0] += 1
            return stk.enter_context(nc.sbuf_tensor("%s_u%d" % (name, uid[0]), shape, dtype))
        _pb = [st.enter_context(nc.psum_tensor("pb%d" % i, [128, 512], F32)) for i in range(4)]
        PST = st.enter_context(nc.psum_tensor("pst", [128, 2, 512], F32))
        _pb6 = st.enter_context(nc.psum_tensor("pb6", [128, 512], F32))
        PB = [t[:] for t in _pb] + [PST[:, 0, :], PST[:, 1, :], _pb6[:]]
        PT = st.enter_context(nc.psum_tensor("pt", [128, 1024], BF16))
        par = sbuf(st, "par", [128, NPAR], F32)
        ones_bf = sbuf(st, "ones_bf", [128, 128], BF16)
        blk_bf = sbuf(st, "blk_bf", [128, 128], BF16)
        id_bf = sbuf(st, "id_bf", [128, 128], BF16)
        tri_bf = sbuf(st, "tri_bf", [128, 128], BF16)
        swab = sbuf(st, "swab", [128, 8, 2, 128], F32)
        dtab = sbuf(st, "dtab", [128, 4, NB], F32)
        dist = sbuf(st, "dist", [128, 2, 128], F32)
        esink = sbuf(st, "esink", [128, 32], F32)
        lamt = sbuf(st, "lamt", [128, 16], F32)
        ljunk = sbuf(st, "ljunk", [128, 64], F32)
        nhalf = sbuf(st, "nhalf", [128, 512], F32)

        P.dma("sp", lambda e: e.dma_start(out=par[:], in_=par_in), "par", writes=["par"])
        P.op("pool", lambda e: e.memset(ones_bf[:], 1.0), writes=["ones"])
        P.op("pool", lambda e: e.memset(nhalf[:], -0.5), writes=["nhalf"])
        P.op("pool", lambda e: e.affine_select(out=id_bf[:], in_=ones_bf[:], pattern=[[-1, 128]],
                                               compare_op=ALU.is_equal, fill=0.0, base=0, channel_multiplier=1),
             reads=["ones"], writes=["id"])
        P.op("pool", lambda e: e.affine_select(out=tri_bf[:], in_=ones_bf[:], pattern=[[1, 128]],
                                               compare_op=ALU.is_ge, fill=0.0, base=0, channel_multiplier=-1),
             reads=["ones"], writes=["tri"])
        P.op("pool", lambda e: e.memset(blk_bf[:], 0.0), writes=["blk"])
        P.op("pool", lambda e: e.memset(blk_bf[0:64, 0:64], 1.0), reads=["blk"], writes=["blk"])
        P.op("pool", lambda e: e.memset(blk_bf[64:128, 64:128], 1.0), reads=["blk"], writes=["blk"])
        for kb in range(2):
            P.op("pool", lambda e, kb=kb: e.iota(dist[:, kb, :], pattern=[[1, 128]], base=128 - 128 * kb,
                                                 channel_multiplier=-1, allow_small_or_imprecise_dtypes=True),
                 reads=["dist"], writes=["dist"])
        for h in range(8):
            for kb in range(2):
                P.op("pool", lambda e, h=h, kb=kb: e.tensor_scalar(out=swab[:, h, kb, :], in0=dist[:, kb, :],
                                                                     scalar1=-SLOPES[h], scalar2=0.0, op0=ALU.mult, op1=ALU.add),
                     reads=["dist", "swab"], writes=["swab"])
                if kb == 0:
                    P.op("pool", lambda e, h=h: e.affine_select(out=swab[:, h, 0, :], in_=swab[:, h, 0, :], pattern=[[-1, 128]],
                                                                compare_op=ALU.is_gt, fill=NEGM, base=0, channel_multiplier=1),
                         reads=["swab"], writes=["swab"])
                else:
                    P.op("pool", lambda e, h=h: e.affine_select(out=swab[:, h, 1, :], in_=swab[:, h, 1, :], pattern=[[1, 128]],
                                                                compare_op=ALU.is_ge, fill=NEGM, base=0, channel_multiplier=-1),
                         reads=["swab"], writes=["swab"])
        for h in range(4):
            P.op("pool", lambda e, h=h: e.iota(dtab[:, h, :], pattern=[[-128, NB]], base=128, channel_multiplier=1,
                                               allow_small_or_imprecise_dtypes=True), reads=["dtab"], writes=["dtab"])
            P.op("pool", lambda e, h=h: e.tensor_scalar(out=dtab[:, h, :], in0=dtab[:, h, :], scalar1=SLOPES[8 + h], scalar2=0.0,
                                                        op0=ALU.mult, op1=ALU.add), reads=["dtab"], writes=["dtab"])
        P.op("act", lambda e: e.activation(out=esink[:], in_=par[:, PC_SK:PC_SK + 32], func=AF.Exp), reads=["par"], writes=["esink"])
        for l in range(depth):
            b = PC_LM + 256 * l
            for i in range(2):
                P.op("dve", lambda e, l=l, b=b, i=i: e.scalar_tensor_tensor(
                    out=ljunk[:], in0=par[:, b + 128 * i:b + 128 * i + 64], scalar=1.0, in1=par[:, b + 128 * i + 64:b + 128 * i + 128],
                    op0=ALU.mult, op1=ALU.mult, accum_out=lamt[:, 4 * l + i:4 * l + i + 1]),
                    reads=["par", "ljunk"], writes=["ljunk", "lamt"])
            P.op("act", lambda e, l=l: e.activation(out=lamt[:, 4 * l:4 * l + 2], in_=lamt[:, 4 * l:4 * l + 2], func=AF.Exp),
                 reads=["lamt"], writes=["lamt"])
            P.op("dve", lambda e, l=l: e.scalar_tensor_tensor(out=lamt[:, 4 * l + 2:4 * l + 3], in0=lamt[:, 4 * l:4 * l + 1],
                                                               scalar=float(lam_inits[l]), in1=lamt[:, 4 * l + 1:4 * l + 2],
                                                               op0=ALU.add, op1=ALU.subtract), reads=["lamt"], writes=["lamt"])
            P.op("dve", lambda e, l=l: e.tensor_scalar(out=lamt[:, 4 * l + 3:4 * l + 4], in0=lamt[:, 4 * l + 2:4 * l + 3],
                                                        scalar1=-1.0, scalar2=0.0, op0=ALU.mult, op1=ALU.add),
                 reads=["lamt"], writes=["lamt"])
        P.barrier()
        P.emit()

        def norm_group(xt, sq, rsd, ht, gcol0, keys, pbank, pkey, part="all"):
            xk, sqk, rsk, hk = keys
            if part == "all":
                P.op("act", lambda e: e.activation(out=sq[:], in_=xt[:], func=AF.Square), reads=[xk], writes=[sqk])
            elif part == "sq":
                P.op("act", lambda e: e.activation(out=sq[:], in_=xt[:], func=AF.Square), reads=[xk], writes=[sqk])
                return
            P.op("pe", [(lambda e, c=c: e.matmul(pbank[:], lhsT=ones_bf[:], rhs=sq[:, c, :], start=(c == 0), stop=(c == 7)))
                        for c in range(8)], reads=[sqk], writes=[pkey])
            P.op("act", lambda e: e.activation(out=rsd[:], in_=pbank[:], func=AF.Sqrt, scale=1.0 / D, bias=EPS),
                 reads=[pkey], writes=[rsk])
            P.op("dve", lambda e: e.reciprocal(out=rsd[:], in_=rsd[:]), reads=[rsk], writes=[rsk])
            for c in range(8):
                P.op("dve", lambda e, c=c: e.scalar_tensor_tensor(out=ht[:, c, :], in0=xt[:, c, :], scalar=par[:, gcol0 + c:gcol0 + c + 1],
                                                                   in1=rsd[:], op0=ALU.mult, op1=ALU.mult),
                     reads=[xk, rsk], writes=[hk])

        def wload(dst, src, key):
            P.dma("pool", lambda e: e.dma_start(out=dst, in_=src), "w_" + key, writes=[key])

        import os as _os
        PH = _os.environ.get("KPHASES", "A,B1,B2,C1,C2").split(",")
        for l in range(depth):
            x_src = xT_in if l == 0 else xs
            x_dst_final = yT if l == depth - 1 else xs
            lam_init = float(lam_inits[l])

            with ExitStack() as ph:
              if "A" in PH:
                wA = sbuf(ph, "wA", [128, 8, 2432], BF16)
                xT = [sbuf(ph, "xT%d" % i, [128, 8, 512], F32) for i in range(2)]
                sq = sbuf(ph, "sq", [128, 8, 512], BF16)
                rsd = sbuf(ph, "rsd", [128, 512], F32)
                hT = [sbuf(ph, "hT%d" % i, [128, 8, 512], BF16) for i in range(2)]
                sq2 = [sbuf(ph, "sq2_%d" % i, [128, 512], BF16) for i in range(4)]
                rs2 = [sbuf(ph, "rs2_%d" % i, [128, 512], F32) for i in range(2)]
                qn = [sbuf(ph, "qn%d" % i, [128, 512], BF16) for i in range(3)]
                vAs = [sbuf(ph, "vAs%d" % i, [128, 4, 130], BF16) for i in range(2)]
                vDs = [sbuf(ph, "vDs%d" % i, [128, 4, 516], BF16) for i in range(2)]
                wl = chunked(w_in[l])
                for (d0, s0, n, wk) in ((0, 0, 512, "wA0"), (512, 512, 64, "wA1"), (576, 512, 64, "wA1"), (640, 576, 64, "wA1"), (704, 576, 64, "wA1"),
                                        (768, 768, 1024, "wA2"), (1792, 640, 128, "wA3"), (1920, 1792, 512, "wA3")):
                    wload(wA[:, :, d0:d0 + n], wl[:, :, s0:s0 + n], wk)
                for i in range(2):
                    P.op("pool", lambda e, i=i: e.memset(vAs[i][:], 1.0), writes=["vAs%d" % i])
                    P.op("pool", lambda e, i=i: e.memset(vDs[i][:], 1.0), writes=["vDs%d" % i])
                rx, rh, rq2, rqn, rv = Ring("xT", 2), Ring("hT", 2), Ring("q2_", 4), Ring("qn", 3), Ring("vs", 2)
                rpq, rps, rr2 = Ring("pq", 4), Ring("pss", 1), Ring("rs2_", 2)
                xsrc_c = chunked(x_src)
                hslot = {}

                xslot = {}

                def load_x(tg):
                    xi = rx.next()
                    xslot[tg] = xi
                    P.dma("sp", lambda e: e.dma_start(out=xT[xi][:], in_=xsrc_c[:, :, tg * 512:(tg + 1) * 512]),
                          "xT%d" % xi, reads=["xs_%d" % tg], writes=[rx.key(xi)])

                def emit_sq(tg):
                    xi = xslot[tg]
                    norm_group(xT[xi], sq, rsd, None, PC_NM + 8 * l, (rx.key(xi), "sq", "rsd", None), PB[6], "pb6", part="sq")

                def emit_norm(tg):
                    xi = xslot[tg]
                    hi = rh.next()
                    hslot[tg] = hi
                    norm_group(xT[xi], sq, rsd, hT[hi], PC_NM + 8 * l, (rx.key(xi), "sq", "rsd", rh.key(hi)), PB[6], "pb6", part="rest")

                pst_ = {}

                def projA(tg, j):
                    hi = hslot[tg]
                    pi = rpq.next()
                    pq, pqk = PB[pi], rpq.key(pi)
                    P.op("pe", [(lambda e, c=c: e.matmul(pq, lhsT=wA[:, c, j * 128:(j + 1) * 128], rhs=hT[hi][:, c, :],
                                                         start=(c == 0), stop=(c == 7))) for c in range(8)],
                         reads=["wA0" if j < 4 else ("wA1" if j < 6 else "wA2"), rh.key(hi)], writes=[pqk])
                    qi = rq2.next()
                    P.op("act", lambda e: e.activation(out=sq2[qi][:], in_=pq, func=AF.Square), reads=[pqk], writes=[rq2.key(qi)])
                    pst_[(tg, j)] = (pq, pqk, qi)

                def restA(tg, j):
                    pq, pqk, qi = pst_.pop((tg, j))
                    si = rps.next()
                    pss, psk = PB[4 + si], rps.key(si)
                    P.op("pe", lambda e: e.matmul(pss, lhsT=blk_bf[:], rhs=sq2[qi][:], start=True, stop=True),
                         reads=[rq2.key(qi)], writes=[psk])
                    ri = rr2.next()
                    rk = rr2.key(ri)
                    P.op("act", lambda e: e.activation(out=rs2[ri][:], in_=pss, func=AF.Sqrt, scale=1.0 / 64, bias=EPS), reads=[psk], writes=[rk])
                    P.op("dve", lambda e: e.reciprocal(out=rs2[ri][:], in_=rs2[ri][:]), reads=[rk], writes=[rk])
                    if j < 4:
                        gc, dst, r0 = PC_QS + 2 * l, QaT, j * 128
                    elif j < 6:
                        gc, dst, r0 = PC_QS + 2 * l + 1, KaT, (j - 4) * 128
                    elif j < 10:
                        gc, dst, r0 = PC_QD + 2 * l, QdT, (j - 6) * 128
                    else:
                        gc, dst, r0 = PC_QD + 2 * l + 1, KdT, (j - 10) * 128
                    ni = rqn.next()
                    P.op("dve", lambda e: e.scalar_tensor_tensor(out=qn[ni][:], in0=pq, scalar=par[:, gc:gc + 1], in1=rs2[ri][:],
                                                                  op0=ALU.mult, op1=ALU.mult), reads=[pqk, rk], writes=[rqn.key(ni)])
                    P.dma("sp", lambda e: e.dma_start(out=dst[r0:r0 + 128, tg * 512:(tg + 1) * 512], in_=qn[ni][:]),
                          "qn%d" % ni, reads=[rqn.key(ni)], writes=["qk_%d" % tg])

                def vblock(tg, hi, hk, vi, tb):
                    P.op("pe", [(lambda e, c=c: e.matmul(PB[6][:, 0:128], lhsT=hT[hi][:, c, tb * 128:(tb + 1) * 128],
                                                         rhs=wA[:, c, 1792:1920], start=(c == 0), stop=(c == 7))) for c in range(8)],
                         reads=["wA3", hk], writes=["pb6"])
                    P.op("act", lambda e: e.activation(
                        out=vAs[vi][:, tb, :].rearrange("p (k d) -> p k d", k=2)[:, :, 0:64],
                        in_=PB[6][:, 0:128].rearrange("p (k d) -> p k d", k=2), func=AF.Copy),
                        reads=["pb6"], writes=["vAs%d" % vi])
                    P.op("pe", [(lambda e, c=c: e.matmul(PB[5], lhsT=hT[hi][:, c, tb * 128:(tb + 1) * 128],
                                                         rhs=wA[:, c, 1920:2432], start=(c == 0), stop=(c == 7))) for c in range(8)],
                         reads=["wA3", hk], writes=["pb5"])
                    P.op("dve", lambda e: e.tensor_copy(
                        out=vDs[vi][:, tb, :].rearrange("p (h d) -> p h d", h=4)[:, :, 0:128],
                        in_=PB[5].rearrange("p (h d) -> p h d", h=4)), reads=["pb5"], writes=["vDs%d" % vi])

                load_x(0)
                emit_sq(0)
                emit_norm(0)
                for tg in range(NG):
                    hi = hslot[tg]
                    hk = rh.key(hi)
                    if tg + 1 < NG:
                        load_x(tg + 1)
                    vi = rv.next()
                    projA(tg, 0)
                    for j in range(14):
                        if j + 1 < 14:
                            projA(tg, j + 1)
                        restA(tg, j)
                        if j in (1, 4, 7, 11):
                            vblock(tg, hi, hk, vi, (1, 4, 7, 11).index(j))
                        if j == 5 and tg + 1 < NG:
                            emit_sq(tg + 1)
                        if j == 9 and tg + 1 < NG:
                            emit_norm(tg + 1)
                    P.dma("sp", lambda e, vi=vi, tg=tg: e.dma_start(out=Va[:, 4 * tg:4 * tg + 4, :], in_=vAs[vi][:]),
                          "vAs%d" % vi, reads=["vAs%d" % vi], writes=["va_%d" % tg])
                    for h in range(4):
                        P.dma("sp", lambda e, vi=vi, tg=tg, h=h: e.dma_start(out=Vd[h, :, 4 * tg:4 * tg + 4, :],
                                                                             in_=vDs[vi][:, :, h * 129:(h + 1) * 129]),
                              "vDs%d" % vi, reads=["vDs%d" % vi], writes=["vd_%d" % tg])
                P.barrier()
                P.emit()

            with ExitStack() as ph:
              if "B1" in PH:
                kA = [sbuf(ph, "kA%d" % i, [128, 2, 640], BF16) for i in range(2)]
                vA = [sbuf(ph, "vA%d" % i, [128, 5, 130], BF16) for i in range(2)]
                qA = [sbuf(ph, "qA%d" % i, [128, 4, 512], BF16) for i in range(2)]
                tmp = [sbuf(ph, "tmp%d" % i, [128, 256], F32) for i in range(3)]
                Es = [sbuf(ph, "Es%d" % i, [128, 256], BF16) for i in range(3)]
                den = [sbuf(ph, "den%d" % i, [128, 1], F32) for i in range(3)]
                oAs = [sbuf(ph, "oAs%d" % i, [128, 128], BF16) for i in range(2)]
                oaT = [sbuf(ph, "oaT%d" % i, [128, 4, 512], BF16) for i in range(2)]
                rin, rt, rE, rden, roA, roT = Ring("swin", 2), Ring("tmp", 3), Ring("Es", 3), Ring("den", 3), Ring("oAs", 2), Ring("oaT", 2)
                rS, rO = Ring("pS", 3), Ring("pO", 2)
                pSb = [PB[0], PB[1], PB[3]]
                pOb = [PB[2], PB[4]]
                KaT_c = KaT.rearrange("(k p) t -> p k t", p=128)
                QaT_c = chunked(QaT)
                OaT_c = chunked(OaT)
                inslot = {}

                def load_in(tg):
                    ii = rin.next()
                    inslot[tg] = ii
                    ik = rin.key(ii)
                    if tg == 0:
                        P.op("pool", lambda e: e.memset(kA[ii][:, :, 0:128], 0.0), writes=[ik])
                        P.op("pool", lambda e: e.memset(vA[ii][:, 0, :], 0.0), reads=[ik], writes=[ik])
                        P.dma("sp", lambda e: e.dma_start(out=kA[ii][:, :, 128:640], in_=KaT_c[:, :, 0:512]), "swin%d" % ii, writes=[ik])
                        P.dma("sp", lambda e: e.dma_start(out=vA[ii][:, 1:5, :], in_=Va[:, 0:4, :]), "swin%d" % ii, writes=[ik])
                    else:
                        P.dma("sp", lambda e: e.dma_start(out=kA[ii][:], in_=KaT_c[:, :, tg * 512 - 128:tg * 512 + 512]), "swin%d" % ii, writes=[ik])
                        P.dma("sp", lambda e: e.dma_start(out=vA[ii][:], in_=Va[:, 4 * tg - 1:4 * tg + 4, :]), "swin%d" % ii, writes=[ik])
                    P.dma("sp", lambda e: e.dma_start(out=qA[ii][:], in_=QaT_c[:, :, tg * 512:(tg + 1) * 512]), "swin%d" % ii, writes=[ik])

                sst = {}

                def scB(tg, it):
                    qc, n, hh = it
                    ii = inslot[tg]
                    ik = rin.key(ii)
                    kv = qc // 2
                    h = 2 * qc + hh
                    pr = slice(64 * hh, 64 * hh + 64)
                    si = rS.next()
                    pS = pSb[si][:, 0:256]
                    psk = rS.key(si)
                    P.op("pe", [(lambda e, kb=kb: e.matmul(pS[:, kb * 128:(kb + 1) * 128], lhsT=kA[ii][pr, kv, (n + kb) * 128:(n + kb + 1) * 128],
                                                           rhs=qA[ii][pr, qc, n * 128:(n + 1) * 128], start=True, stop=True)) for kb in range(2)],
                         reads=[ik], writes=[psk])
                    ti = rt.next()
                    P.op("dve", lambda e: e.scalar_tensor_tensor(out=tmp[ti][:], in0=pS, scalar=0.125, in1=swab[:, h, :, :].rearrange("p a b -> p (a b)"),
                                                                  op0=ALU.mult, op1=ALU.add), reads=[psk], writes=[rt.key(ti)])
                    ei = rE.next()
                    P.op("act", lambda e: e.activation(out=Es[ei][:], in_=tmp[ti][:], func=AF.Exp), reads=[rt.key(ti)], writes=[rE.key(ei)])
                    sst[(tg, it)] = ei

                def pvB(tg, it, oi, oti):
                    qc, n, hh = it
                    ii = inslot[tg]
                    ik = rin.key(ii)
                    kv = qc // 2
                    h = 2 * qc + hh
                    ei = sst.pop((tg, it))
                    pi = rO.next()
                    pO = pOb[pi][:, 0:65]
                    pok = rO.key(pi)
                    P.op("pe", [(lambda e, kb=kb: e.matmul(pO, lhsT=Es[ei][:, kb * 128:(kb + 1) * 128], rhs=vA[ii][:, n + kb, kv * 65:kv * 65 + 65],
                                                           start=(kb == 0), stop=(kb == 1))) for kb in range(2)], reads=[rE.key(ei), ik], writes=[pok])
                    di = rden.next()
                    dk = rden.key(di)
                    P.op("dve", lambda e: e.tensor_scalar(out=den[di][:], in0=pO[:, 64:65], scalar1=esink[:, 8 * l + h:8 * l + h + 1], scalar2=None, op0=ALU.add),
                         reads=[pok], writes=[dk])
                    P.op("dve", lambda e: e.reciprocal(out=den[di][:], in_=den[di][:]), reads=[dk], writes=[dk])
                    P.op("dve", lambda e: e.tensor_scalar(out=oAs[oi][:, 64 * hh:64 * hh + 64], in0=pO[:, 0:64], scalar1=den[di][:, 0:1], scalar2=None, op0=ALU.mult),
                         reads=[pok, dk], writes=[roA.key(oi)])
                    if hh == 1:
                        pendT.append((oi, oti, qc, n))

                pendT = []

                def flushT():
                    while pendT:
                        oi, oti, qc, n = pendT.pop(0)
                        pT = PT[:, 0:128]
                        P.op("pe", lambda e, oi=oi: e.transpose(pT, oAs[oi][:], id_bf[:]), reads=[roA.key(oi)], writes=["pTb1"])
                        P.op("act", lambda e, oti=oti, qc=qc, n=n: e.activation(out=oaT[oti][:, qc, n * 128:(n + 1) * 128], in_=pT, func=AF.Copy),
                             reads=["pTb1"], writes=[roT.key(oti)])

                load_in(0)
                its = [(qc, n, hh) for qc in range(4) for n in range(4) for hh in range(2)]
                for tg in range(NG):
                    if tg + 1 < NG:
                        load_in(tg + 1)
                    oti = roT.next()
                    scB(tg, its[0])
                    scB(tg, its[1])
                    oi = 0
                    for k, it in enumerate(its):
                        if k + 2 < len(its):
                            scB(tg, its[k + 2])
                        if it[2] == 0:
                            oi = roA.next()
                        had = len(pendT) > 0
                        pvB(tg, it, oi, oti)
                        if had:
                            flushT()
                    flushT()
                    P.dma("sp", lambda e, oti=oti, tg=tg: e.dma_start(out=OaT_c[:, :, tg * 512:(tg + 1) * 512], in_=oaT[oti][:]),
                          "oaT%d" % oti, reads=[roT.key(oti)], writes=["oa_%d" % tg])
                P.barrier()
                P.emit()

            with ExitStack() as ph:
              if "B2" in PH:
                Kh = [sbuf(ph, "Kh%d" % i, [128, S], BF16) for i in range(2)]
                Vh = [sbuf(ph, "Vh%d" % i, [128, NB, 129], BF16) for i in range(2)]
                qD = [sbuf(ph, "qD%d" % i, [128, 2, 256], BF16) for i in range(3)]
                Ed = [sbuf(ph, "Ed%d" % i, [128, 2, 256], BF16) for i in range(3)]
                rz = [sbuf(ph, "rz%d" % i, [128, 4], F32) for i in range(2)]
                o0 = [sbuf(ph, "o0_%d" % i, [128, 128], F32) for i in range(2)]
                oo = [sbuf(ph, "oo_%d" % i, [128, 128], F32) for i in range(2)]
                jk = sbuf(ph, "jk", [128, 128], F32)
                ob = [sbuf(ph, "ob_%d" % i, [128, 128], BF16) for i in range(2)]
                obT = [sbuf(ph, "obT%d" % i, [128, 256], BF16) for i in range(2)]
                accS = [[sbuf(ph, "accS%d_%d" % (i, k), [128, 129], F32) for k in range(4)] for i in range(2)]
                rKV, rq, rEd, rfin, robT = Ring("KV", 2), Ring("qD", 3), Ring("Ed", 3), Ring("fin", 2), Ring("obT", 2)
                rST, rT, rAS = Ring("pST", 2), Ring("pTd", 4), Ring("accS", 2)
                acck = ["acc%d" % i for i in range(4)]
                kvslot, qslot, est = {}, {}, {}

                def load_kv(h):
                    ki = rKV.next()
                    kvslot[h] = ki
                    kk = rKV.key(ki)
                    P.dma("sp", lambda e: e.dma_start(out=Kh[ki][:], in_=KdT[h * 128:(h + 1) * 128, :]), "KV%d" % ki, writes=[kk])
                    P.dma("sp", lambda e: e.dma_start(out=Vh[ki][:], in_=Vd[h]), "KV%d" % ki, writes=[kk])

                for i in range(3):
                    P.op("pool", lambda e, i=i: e.memset(qD[i][:], 0.0), writes=["qD%d" % i])

                def load_q(h, g2):
                    qi = rq.next()
                    qslot[(h, g2)] = qi
                    for c in range(2):
                        P.dma("sp", lambda e, c=c: e.dma_start(out=qD[qi][64 * c:64 * c + 64, c, :],
                                                             in_=QdT[h * 128 + 64 * c:h * 128 + 64 * c + 64, g2 * 256:(g2 + 1) * 256]),
                              "qD%d" % qi, writes=[rq.key(qi)])

                def scores(h, g2, m):
                    ki, qi = kvslot[h], qslot[(h, g2)]
                    kk, qk = rKV.key(ki), rq.key(qi)
                    c0 = 0 if m <= 2 * g2 else 128
                    dd = 2 * g2 - m
                    sti = rST.next()
                    stk = rST.key(sti)
                    pst = PB[4 + sti].rearrange("p (c q) -> p c q", c=2)
                    if c0 == 0:
                        P.op("pe", lambda e: e.matmul(PB[4 + sti], lhsT=Kh[ki][:, m * 128:(m + 1) * 128],
                                                      rhs=qD[qi][:].rearrange("p c q -> p (c q)"), start=True, stop=True),
                             reads=[kk, qk], writes=[stk])
                    else:
                        P.op("pe", [(lambda e, c=c: e.matmul(pst[:, c, c0:256], lhsT=Kh[ki][:, m * 128:(m + 1) * 128],
                                                             rhs=qD[qi][:, c, c0:256], start=True, stop=True)) for c in range(2)],
                             reads=[kk, qk], writes=[stk])
                    ei = rEd.next()
                    ek = rEd.key(ei)
                    P.op("act", lambda e: e.activation(
                        out=Ed[ei][:, :, c0:256], in_=pst[:, :, c0:256], func=AF.Exp,
                        scale=0.125, bias=dtab[:, h, dd + 1:dd + 2]), reads=[stk], writes=[ek])
                    if m >= 2 * g2:
                        for c in range(2):
                            P.op("pool", lambda e, c=c: e.tensor_tensor(
                                out=Ed[ei][:, c, c0:c0 + 128], in0=Ed[ei][:, c, c0:c0 + 128], in1=tri_bf[:], op=ALU.mult),
                                reads=[ek], writes=[ek])
                    est[(h, g2, m)] = (ei, c0 // 128)

                def pv(h, g2, m):
                    ei, nq0 = est.pop((h, g2, m))
                    ki = kvslot[h]
                    fns, wr = [], []
                    for c in range(2):
                        for nn in range(nq0, 2):
                            fns.append(lambda e, c=c, nn=nn: e.matmul(
                                PB[2 * c + nn][:, 0:129], lhsT=Ed[ei][:, c, nn * 128:(nn + 1) * 128], rhs=Vh[ki][:, m, :],
                                start=(m == 0), stop=(m == 2 * g2 + nn)))
                            wr.append(acck[2 * c + nn])
                    P.op("pe", fns, reads=[rEd.key(ei), rKV.key(ki)], writes=wr)

                pending = []

                def finalize(h, g2):
                    finalize2()
                    si = rAS.next()
                    sk = rAS.key(si)
                    A = accS[si]
                    for k in range(4):
                        P.op("dve", lambda e, k=k: e.tensor_copy(out=A[k][:], in_=PB[k][:, 0:129]), reads=[acck[k]], writes=[sk + "_%d" % k])
                    oi = robT.next()
                    obk = robT.key(oi)
                    fis = []
                    for nn in range(2):
                        fi = rfin.next()
                        fk = rfin.key(fi)
                        a0, a1 = A[nn], A[2 + nn]
                        k0, k1 = sk + "_%d" % nn, sk + "_%d" % (2 + nn)
                        P.op("dve", lambda e, fi=fi, a0=a0: e.reciprocal(out=rz[fi][:, 0:1], in_=a0[:, 128:129]), reads=[k0], writes=[fk])
                        P.op("dve", lambda e, fi=fi, a1=a1: e.reciprocal(out=rz[fi][:, 1:2], in_=a1[:, 128:129]), reads=[k1, fk], writes=[fk])
                        P.op("dve", lambda e, fi=fi: e.tensor_tensor(out=rz[fi][:, 2:3], in0=rz[fi][:, 1:2], in1=lamt[:, 4 * l + 3:4 * l + 4], op=ALU.mult),
                             reads=[fk], writes=[fk])
                        P.op("dve", lambda e, fi=fi, a0=a0: e.tensor_scalar(out=o0[fi][:], in0=a0[:, 0:128], scalar1=rz[fi][:, 0:1], scalar2=None, op0=ALU.mult),
                             reads=[k0, fk], writes=[fk])
                        P.op("dve", lambda e, fi=fi, a1=a1: e.scalar_tensor_tensor(out=oo[fi][:], in0=a1[:, 0:128], scalar=rz[fi][:, 2:3], in1=o0[fi][:],
                                                                                   op0=ALU.mult, op1=ALU.add), reads=[k1, fk], writes=[fk])
                        P.op("dve", lambda e, fi=fi: e.scalar_tensor_tensor(out=jk[:], in0=oo[fi][:], scalar=1.0, in1=oo[fi][:], op0=ALU.mult, op1=ALU.mult,
                                                                            accum_out=rz[fi][:, 3:4]), reads=[fk, "jk"], writes=[fk, "jk"])
                        P.op("dve", lambda e, fi=fi: e.tensor_scalar(out=rz[fi][:, 3:4], in0=rz[fi][:, 3:4], scalar1=1.0 / 128, scalar2=EPS, op0=ALU.mult, op1=ALU.add),
                             reads=[fk], writes=[fk])
                        P.op("pool", lambda e, fi=fi: e.tensor_tensor(out=rz[fi][:, 3:4], in0=rz[fi][:, 3:4], in1=nhalf[:, 0:1], op=ALU.pow),
                             reads=[fk], writes=[fk])
                        P.op("dve", lambda e, fi=fi: e.tensor_scalar(out=ob[fi][:], in0=oo[fi][:], scalar1=rz[fi][:, 3:4], scalar2=None, op0=ALU.mult),
                             reads=[fk], writes=[fk])
                        fis.append((fi, fk))
                    pending.append((h, g2, oi, obk, fis))

                def finalize2():
                    while pending:
                        h, g2, oi, obk, fis = pending.pop(0)
                        for nn, (fi, fk) in enumerate(fis):
                            ti2 = 0
                            pT = PT[:, ti2 * 128:(ti2 + 1) * 128]
                            P.op("pe", lambda e, pT=pT, fi=fi: e.transpose(pT, ob[fi][:], id_bf[:]), reads=[fk], writes=[rT.key(ti2)])
                            P.op("dve", lambda e, pT=pT, nn=nn, oi=oi: e.tensor_scalar(
                                out=obT[oi][:, nn * 128:(nn + 1) * 128], in0=pT, scalar1=par[:, PC_SL + l:PC_SL + l + 1], scalar2=1.0 - lam_init,
                                op0=ALU.mult, op1=ALU.mult), reads=[rT.key(ti2)], writes=[obk])
                        P.dma("sp", lambda e, h=h, g2=g2, oi=oi: e.dma_start(out=ObT[h * 128:(h + 1) * 128, g2 * 256:(g2 + 1) * 256], in_=obT[oi][:]),
                              "obT%d" % oi, reads=[obk], writes=["ob_%d_%d" % (h, g2)])

                items = [(h, g2, m) for h in range(4) for g2 in range(NG2) for m in range(2 * g2 + 2)]
                groups = [(h, g2) for h in range(4) for g2 in range(NG2)]
                gidx = {g: i for i, g in enumerate(groups)}

                def prep(i):
                    h, g2, m = items[i]
                    if m == 0:
                        gi = gidx[(h, g2)]
                        if gi == 0:
                            load_kv(0)
                            load_q(0, 0)
                        if g2 == 1 and h + 1 < 4:
                            load_kv(h + 1)
                        if gi + 1 < len(groups):
                            load_q(*groups[gi + 1])
                    scores(h, g2, m)

                prep(0)
                since = 0
                for i in range(len(items)):
                    if i + 1 < len(items):
                        prep(i + 1)
                    h, g2, m = items[i]
                    pv(h, g2, m)
                    since += 1
                    if pending and since >= 6:
                        finalize2()
                    if m == 2 * g2 + 1:
                        finalize(h, g2)
                        since = 0
                finalize2()
                P.barrier()
                P.emit()

            with ExitStack() as ph:
              if "C1" in PH:
                wG = sbuf(ph, "wG", [128, 8, 2048], BF16)
                wB = sbuf(ph, "wB", [128, 2, 4, 1024], BF16)
                wO = sbuf(ph, "wO", [128, 8, 1024], BF16)
                xT = [sbuf(ph, "xT%d" % i, [128, 8, 512], F32) for i in range(2)]
                sq = sbuf(ph, "sq", [128, 8, 512], BF16)
                rsd = sbuf(ph, "rsd", [128, 512], F32)
                hT = [sbuf(ph, "hT%d" % i, [128, 8, 512], BF16) for i in range(2)]
                gT = sbuf(ph, "gT", [128, 16, 512], BF16)
                oab = [sbuf(ph, "oab%d" % i, [128, 8, 512], BF16) for i in range(2)]
                m1 = [sbuf(ph, "m1_%d" % i, [128, 512], F32) for i in range(2)]
                m2 = [sbuf(ph, "m2_%d" % i, [128, 512], F32) for i in range(2)]
                mT = sbuf(ph, "mT", [128, 8, 512], BF16)
                wl = chunked(w_in[l])
                for q4 in range(4):
                    wload(wG[:, :, q4 * 512:(q4 + 1) * 512], wl[:, :, 2304 + q4 * 512:2304 + (q4 + 1) * 512], "wG%d" % q4)
                for n in range(2):
                    wload(wB[:, n, :, :], w_br[l, n].rearrange("(k p) d -> p k d", p=128), "wB")
                wol = chunked(w_o[l])
                for q2 in range(2):
                    wload(wO[:, :, q2 * 512:(q2 + 1) * 512], wol[:, :, q2 * 512:(q2 + 1) * 512], "wO")
                rx, roab, rm = Ring("xT", 2), Ring("oab", 2), Ring("m", 2)
                rpq = Ring("pq", 2)
                rpa = Ring("pa", 2)
                xsrc_c = chunked(x_src)
                xs_c = chunked(xs)
                OaT_c, ObT_c = chunked(OaT), chunked(ObT)
                c1slot = {}

                def pro_c1(tg):
                    xi = rx.next()
                    xk = rx.key(xi)
                    P.dma("sp", lambda e: e.dma_start(out=xT[xi][:], in_=xsrc_c[:, :, tg * 512:(tg + 1) * 512]),
                          "xT%d" % xi, reads=["xs_%d" % tg], writes=[xk])
                    ai = roab.next()
                    ak = roab.key(ai)
                    P.dma("sp", lambda e: e.dma_start(out=oab[ai][:, 0:4, :], in_=OaT_c[:, :, tg * 512:(tg + 1) * 512]), "oab%d" % ai, writes=[ak])
                    P.dma("sp", lambda e: e.dma_start(out=oab[ai][:, 4:8, :], in_=ObT_c[:, :, tg * 512:(tg + 1) * 512]), "oab%d" % ai, writes=[ak])
                    c1slot[tg] = (xi, ai)

                rhc = Ring("hTc", 2)
                hcs = {}

                def sq_c1(tg):
                    xi_, _ = c1slot[tg]
                    norm_group(xT[xi_], sq, rsd, None, PC_NM + 8 * l, (rx.key(xi_), "sq", "rsd", None), PB[6], "pb6", part="sq")

                def norm_c1(tg):
                    xi_, _ = c1slot[tg]
                    hi_ = rhc.next()
                    hcs[tg] = hi_
                    norm_group(xT[xi_], sq, rsd, hT[hi_], PC_NM + 8 * l, (rx.key(xi_), "sq", "rsd", rhc.key(hi_)), PB[6], "pb6", part="rest")

                pro_c1(0)
                sq_c1(0)
                norm_c1(0)
                for tg in range(NG):
                    xi, ai = c1slot[tg]
                    xk, ak = rx.key(xi), roab.key(ai)
                    hi = hcs[tg]
                    hk = rhc.key(hi)
                    if tg + 1 < NG:
                        pro_c1(tg + 1)
                    for j in range(16):
                        if j == 6 and tg + 1 < NG:
                            sq_c1(tg + 1)
                        if j == 10 and tg + 1 < NG:
                            norm_c1(tg + 1)
                        pi = rpq.next()
                        pq, pqk = PB[pi], rpq.key(pi)
                        P.op("pe", [(lambda e, c=c, j=j, pq=pq, hi=hi: e.matmul(pq[:], lhsT=wG[:, c, j * 128:(j + 1) * 128], rhs=hT[hi][:, c, :],
                                                                                  start=(c == 0), stop=(c == 7))) for c in range(8)],
                             reads=["wG%d" % (j // 4), hk], writes=[pqk])
                        P.op("act", lambda e, j=j, pq=pq: e.activation(out=gT[:, j, :], in_=pq[:], func=AF.Sigmoid,
                                                                       bias=par[:, PC_BG + 16 * l + j:PC_BG + 16 * l + j + 1]),
                             reads=[pqk], writes=["gT"])
                    for j in range(8):
                        pi = rpa.next()
                        pa, pb_ = PB[2 + 2 * pi], PB[3 + 2 * pi]
                        pak = rpa.key(pi)
                        P.op("pe", [(lambda e, kc=kc, j=j, pa=pa, ai=ai: e.matmul(pa[:], lhsT=wB[:, 0, kc, j * 128:(j + 1) * 128], rhs=oab[ai][:, kc, :],
                                                                                     start=(kc == 0), stop=(kc == 3))) for kc in range(4)] +
                                   [(lambda e, kc=kc, j=j, pb_=pb_, ai=ai: e.matmul(pb_[:], lhsT=wB[:, 1, kc, j * 128:(j + 1) * 128], rhs=oab[ai][:, 4 + kc, :],
                                                                                       start=(kc == 0), stop=(kc == 3))) for kc in range(4)],
                             reads=["wB", ak], writes=[pak])
                        mi = rm.next()
                        mk = rm.key(mi)
                        P.op("dve", lambda e, mi=mi, pa=pa, j=j: e.tensor_tensor(out=m1[mi][:], in0=pa[:], in1=gT[:, j, :], op=ALU.mult),
                             reads=[pak, "gT"], writes=[mk])
                        P.op("dve", lambda e, mi=mi, pb_=pb_, j=j: e.tensor_tensor(out=m2[mi][:], in0=pb_[:], in1=gT[:, 8 + j, :], op=ALU.mult),
                             reads=[pak, "gT", mk], writes=[mk])
                        P.op("pool", lambda e, mi=mi, j=j: e.tensor_tensor(out=mT[:, j, :], in0=m1[mi][:], in1=m2[mi][:], op=ALU.add),
                             reads=[mk], writes=["mT"])
                    for j in range(8):
                        pi = rpq.next()
                        pq, pqk = PB[pi], rpq.key(pi)
                        P.op("pe", [(lambda e, c=c, j=j, pq=pq: e.matmul(pq[:], lhsT=wO[:, c, j * 128:(j + 1) * 128], rhs=mT[:, c, :],
                                                                           start=(c == 0), stop=(c == 7))) for c in range(8)],
                             reads=["wO", "mT"], writes=[pqk])
                        P.op("dve", lambda e, j=j, pq=pq, xi=xi: e.tensor_tensor(out=xT[xi][:, j, :], in0=xT[xi][:, j, :], in1=pq[:], op=ALU.add),
                             reads=[pqk, xk], writes=[xk])
                    P.dma("sp", lambda e, xi=xi, tg=tg: e.dma_start(out=xs_c[:, :, tg * 512:(tg + 1) * 512], in_=xT[xi][:]),
                          "xo%d" % xi, reads=[xk], writes=["xs_%d" % tg])
                P.barrier()
                P.emit()

            for half in range(2):
                with ExitStack() as ph:
                  if "C2" in PH:
                    wI = sbuf(ph, "wI", [128, 8, 2, 1408], BF16)
                    wF = sbuf(ph, "wF", [128, 11, 1024], BF16)
                    xT = [sbuf(ph, "xT%d" % i, [128, 8, 512], F32) for i in range(2)]
                    sq = sbuf(ph, "sq", [128, 8, 512], BF16)
                    rsd = sbuf(ph, "rsd", [128, 512], F32)
                    h2 = [sbuf(ph, "h2_%d" % i, [128, 8, 512], BF16) for i in range(2)]
                    sil = [sbuf(ph, "sil%d" % i, [128, 512], F32) for i in range(2)]
                    aT = sbuf(ph, "aT", [128, 11, 512], BF16)
                    wil = chunked(w_fi[l])
                    for gu in range(2):
                        wload(wI[:, :, gu, :], wil[:, :, gu * HID + half * 1408:gu * HID + (half + 1) * 1408], "wI")
                    wfl = w_fo[l, half * 1408:(half + 1) * 1408, :].rearrange("(k p) d -> p k d", p=128)
                    for q2 in range(2):
                        wload(wF[:, :, q2 * 512:(q2 + 1) * 512], wfl[:, :, q2 * 512:(q2 + 1) * 512], "wF")
                    rx, rh2, rsl = Ring("xT", 2), Ring("h2_", 2), Ring("sil", 2)
                    rpg, rpo = Ring("pg", 2), Ring("po", 2)
                    xs_c = chunked(xs)
                    H2_c = chunked(H2T)
                    dst_c = chunked(x_dst_final if half == 1 else xs)
                    c2slot = {}

                    def pro_c2(tg):
                        xi = rx.next()
                        xk = rx.key(xi)
                        P.dma("sp", lambda e: e.dma_start(out=xT[xi][:], in_=xs_c[:, :, tg * 512:(tg + 1) * 512]),
                              "xT%d" % xi, reads=["xs_%d" % tg], writes=[xk])
                        hi = rh2.next()
                        hk = rh2.key(hi)
                        if half == 1:
                            P.dma("sp", lambda e: e.dma_start(out=h2[hi][:], in_=H2_c[:, :, tg * 512:(tg + 1) * 512]),
                                  "h2i%d" % hi, reads=["h2_%d" % tg], writes=[hk])
                        c2slot[tg] = (xi, hi)

                    def sq_c2(tg):
                        xi_, hi_ = c2slot[tg]
                        norm_group(xT[xi_], sq, rsd, None, PC_NF + 8 * l, (rx.key(xi_), "sq", "rsd", None), PB[6], "pb6", part="sq")

                    def norm_c2(tg):
                        xi_, hi_ = c2slot[tg]
                        norm_group(xT[xi_], sq, rsd, h2[hi_], PC_NF + 8 * l, (rx.key(xi_), "sq", "rsd", rh2.key(hi_)), PB[6], "pb6", part="rest")
                        P.dma("sp", lambda e: e.dma_start(out=H2_c[:, :, tg * 512:(tg + 1) * 512], in_=h2[hi_][:]),
                              "h2o%d" % hi_, reads=[rh2.key(hi_)], writes=["h2_%d" % tg])

                    pro_c2(0)
                    for tg in range(NG):
                        xi, hi = c2slot[tg]
                        xk, hk = rx.key(xi), rh2.key(hi)
                        if half == 0 and tg == 0:
                            sq_c2(0)
                            norm_c2(0)
                        if tg + 1 < NG:
                            pro_c2(tg + 1)
                        for jj in range(11):
                            if half == 0 and jj == 3 and tg + 1 < NG:
                                sq_c2(tg + 1)
                            if half == 0 and jj == 6 and tg + 1 < NG:
                                norm_c2(tg + 1)
                            pi = rpg.next()
                            pg, pu = PB[2 * pi], PB[2 * pi + 1]
                            pgk = rpg.key(pi)
                            P.op("pe", [(lambda e, c=c, jj=jj, pg=pg, hi=hi: e.matmul(pg[:], lhsT=wI[:, c, 0, jj * 128:(jj + 1) * 128], rhs=h2[hi][:, c, :],
                                                                                         start=(c == 0), stop=(c == 7))) for c in range(8)] +
                                       [(lambda e, c=c, jj=jj, pu=pu, hi=hi: e.matmul(pu[:], lhsT=wI[:, c, 1, jj * 128:(jj + 1) * 128], rhs=h2[hi][:, c, :],
                                                                                         start=(c == 0), stop=(c == 7))) for c in range(8)],
                                 reads=["wI", hk], writes=[pgk])
                            si = rsl.next()
                            P.op("act", lambda e, si=si, pg=pg: e.activation(out=sil[si][:], in_=pg[:], func=AF.Silu), reads=[pgk], writes=[rsl.key(si)])
                            P.op("dve", lambda e, si=si, pu=pu, jj=jj: e.tensor_tensor(out=aT[:, jj, :], in0=sil[si][:], in1=pu[:], op=ALU.mult),
                                 reads=[pgk, rsl.key(si)], writes=["aT"])
                        for j in range(8):
                            pi = rpo.next()
                            po, pok = PB[4 + pi], rpo.key(pi)
                            P.op("pe", [(lambda e, jj=jj, j=j, po=po: e.matmul(po[:], lhsT=wF[:, jj, j * 128:(j + 1) * 128], rhs=aT[:, jj, :],
                                                                                 start=(jj == 0), stop=(jj == 10))) for jj in range(11)],
                                 reads=["wF", "aT"], writes=[pok])
                            P.op("dve", lambda e, j=j, po=po, xi=xi: e.tensor_tensor(out=xT[xi][:, j, :], in0=xT[xi][:, j, :], in1=po[:], op=ALU.add),
                                 reads=[pok, xk], writes=[xk])
                        P.dma("sp", lambda e, xi=xi, tg=tg: e.dma_start(out=dst_c[:, :, tg * 512:(tg + 1) * 512], in_=xT[xi][:]),
                              "xo%d" % xi, reads=[xk], writes=["xs_%d" % tg])
                    P.barrier()
                    P.emit()
        build_nc.n_ins = P.n_ins
    return nc


def pack_params(b_gate, norm_mix, norm_ffn, qk_norm_swa, qk_norm_diff, attn_sinks, diff_lambda, diff_subln):
    depth = b_gate.shape[0]
    par = np.zeros((128, NPAR), np.float32)
    f = lambda a: np.asarray(a, np.float32)
    par[:, PC_BG:PC_BG + 16 * depth] = f(b_gate).reshape(depth, 16, 128).transpose(2, 0, 1).reshape(128, 16 * depth)
    par[:, PC_NM:PC_NM + 8 * depth] = f(norm_mix).reshape(depth, 8, 128).transpose(2, 0, 1).reshape(128, 8 * depth)
    par[:, PC_NF:PC_NF + 8 * depth] = f(norm_ffn).reshape(depth, 8, 128).transpose(2, 0, 1).reshape(128, 8 * depth)
    qs = f(qk_norm_swa).reshape(depth * 2, 64).T
    par[:, PC_QS:PC_QS + 2 * depth] = np.concatenate([qs, qs], axis=0)
    qd = f(qk_norm_diff).reshape(depth * 2, 64).T
    par[:, PC_QD:PC_QD + 2 * depth] = np.concatenate([qd, qd], axis=0)
    par[:, PC_SK:PC_SK + 8 * depth] = np.broadcast_to(f(attn_sinks).reshape(1, 8 * depth), (128, 8 * depth))
    par[:, PC_SL:PC_SL + depth] = f(diff_subln).T
    par[:, PC_LM:PC_LM + 256 * depth] = np.broadcast_to(f(diff_lambda).reshape(1, 256 * depth), (128, 256 * depth))
    return par


def run_model(x, w_in, b_gate, w_branch, w_o, norm_mix, norm_ffn, qk_norm_swa, qk_norm_diff,
              attn_sinks, diff_lambda, diff_subln, w_ffn_in, w_ffn_out, runner=None):
    x = np.asarray(x, np.float32)
    B, S, _ = x.shape
    depth = np.asarray(w_in).shape[0]
    lam_inits = [0.8 - 0.6 * math.exp(-0.3 * l) for l in range(depth)]
    nc = build_nc(S, depth, lam_inits)
    par = pack_params(b_gate, norm_mix, norm_ffn, qk_norm_swa, qk_norm_diff, attn_sinks, diff_lambda, diff_subln)
    ws = {"w_in": np.ascontiguousarray(w_in, np.float32), "w_branch": np.ascontiguousarray(w_branch, np.float32),
          "w_o": np.ascontiguousarray(w_o, np.float32), "w_ffn_in": np.ascontiguousarray(w_ffn_in, np.float32),
          "w_ffn_out": np.ascontiguousarray(w_ffn_out, np.float32), "par": par}
    n_cores = 8
    in_maps = []
    for c in range(n_cores):
        b = (c * B) // n_cores
        m = {"xT": np.ascontiguousarray(x[b].T)}
        m.update(ws)
        in_maps.append(m)
    if runner is None:
        res = run_bass_kernel_spmd(nc, in_maps, core_ids=list(range(n_cores))).results
    else:
        res = runner(nc, in_maps)
    out = np.empty((B, S, D), np.float32)
    per = n_cores // B
    for b in range(B):
        out[b] = res[b * per]["yT"].T
    return out


def kernel(**inputs):
    return run_model(**inputs)
```

```python
import math
from contextlib import ExitStack
import numpy as np
import concourse.bass as bass
import concourse.mybir as mybir
from concourse.bass_utils import run_bass_kernel_spmd

F32 = mybir.dt.float32
BF16 = mybir.dt.bfloat16
AF = mybir.ActivationFunctionType
ALU = mybir.AluOpType

D = 1024
NCH = 8
HID = 2816
IN_COLS = 4352
EPS = 1e-6
NEGM = -30000.0
SLOPES = [2.0 ** (-8.0 * i / 12.0) for i in range(1, 13)]

PC_BG = 0
PC_NM = 64
PC_NF = 96
PC_QS = 128
PC_QD = 136
PC_SK = 144
PC_SL = 176
PC_LM = 180
NPAR = 180 + 1024


class Prog:
    COMPUTE = ("pe", "act", "dve", "pool")

    def __init__(self, nc, stack):
        self.nc = nc
        self.stack = stack
        self.q = {k: [] for k in ("pe", "act", "dve", "pool", "sp")}
        self.sem = {}
        self.cnt = {}
        for e in self.COMPUTE:
            self.sem[e] = stack.enter_context(nc.semaphore("prog_" + e))
            self.cnt[e] = 0
        self.dsem = {}
        self.dcnt = {}
        self.last_write = {}
        self.readers = {}
        self.seen = {k: {} for k in self.q}
        self.n_ins = 0

    def _deps(self, reads, writes):
        deps = []
        for r in reads:
            t = self.last_write.get(r)
            if t is not None:
                deps.append(t)
        for w in writes:
            t = self.last_write.get(w)
            if t is not None:
                deps.append(t)
            deps.extend(self.readers.get(w, ()))
        return deps

    def _emit_waits(self, eng, deps):
        seen = self.seen[eng]
        need = {}
        for (owner, sem, val) in deps:
            if owner == "pe" and eng == "pe":
                continue
            key = id(sem)
            if seen.get(key, 0) >= val:
                continue
            if key not in need or need[key][1] < val:
                need[key] = (sem, val)
        for key, (sem, val) in need.items():
            seen[key] = val
            self.q[eng].append(("wait", sem, val))

    def _commit(self, token, reads, writes):
        for w in writes:
            self.last_write[w] = token
            self.readers[w] = []
        for r in reads:
            self.readers.setdefault(r, []).append(token)

    def op(self, eng, fn, reads=(), writes=()):
        self._emit_waits(eng, self._deps(reads, writes))
        self.cnt[eng] += 1
        token = (eng, self.sem[eng], self.cnt[eng])
        fns = fn if isinstance(fn, (list, tuple)) else [fn]
        for f in fns[:-1]:
            self.q[eng].append(("ins", f, None))
        self.q[eng].append(("ins", fns[-1], self.sem[eng]))
        self.n_ins += len(fns)
        self._commit(token, reads, writes)

    def dma(self, queue, fn, semkey, reads=(), writes=()):
        self._emit_waits(queue, self._deps(reads, writes))
        if semkey not in self.dsem:
            self.dsem[semkey] = self.stack.enter_context(self.nc.semaphore("d_" + str(semkey)))
            self.dcnt[semkey] = 0
        self.dcnt[semkey] += 16
        sem = self.dsem[semkey]
        token = ("dma", sem, self.dcnt[semkey])
        self.q[queue].append(("dma", fn, sem))
        self.n_ins += 1
        self._commit(token, reads, writes)

    def barrier(self):
        toks = [(e, self.sem[e], self.cnt[e]) for e in self.COMPUTE if self.cnt[e] > 0]
        toks += [("dma", self.dsem[k], self.dcnt[k]) for k in self.dsem]
        for eng in self.q:
            self._emit_waits(eng, [t for t in toks if not (t[0] == eng and eng != "pe" and False)])
        self.last_write = {}
        self.readers = {}

    def emit(self):
        nc = self.nc
        q = self.q

        def run(engine, items):
            for it in items:
                if it[0] == "wait":
                    engine.wait_ge(it[1], it[2])
                elif it[0] == "ins":
                    ins = it[1](engine)
                    if it[2] is not None:
                        ins.then_inc(it[2], 1)
                else:
                    it[1](engine).then_inc(it[2], 16)

        with nc.Block() as block:
            @block.tensor
            def _(e):
                run(e, q["pe"])

            @block.scalar
            def _(e):
                run(e, q["act"])

            @block.vector
            def _(e):
                run(e, q["dve"])

            @block.gpsimd
            def _(e):
                run(e, q["pool"])

            @block.sync
            def _(e):
                run(e, q["sp"])
        self.q = {k: [] for k in q}


class Ring:
    def __init__(self, name, n):
        self.name, self.n, self.i = name, n, -1

    def next(self):
        self.i = (self.i + 1) % self.n
        return self.i

    def key(self, i):
        return "%s%d" % (self.name, i)


def build_nc(S, depth, lam_inits):
    NG = S // 512
    NB = S // 128
    NG2 = S // 256
    nc = bass.Bass("TRN2", target_bir_lowering=False)
    dt = lambda name, shape, dtype, kind: nc.dram_tensor(name, shape, dtype, kind=kind).ap()
    xT_in = dt("xT", [D, S], F32, "ExternalInput")
    par_in = dt("par", [128, NPAR], F32, "ExternalInput")
    w_in = dt("w_in", [depth, D, IN_COLS], F32, "ExternalInput")
    w_br = dt("w_branch", [depth, 2, 512, D], F32, "ExternalInput")
    w_o = dt("w_o", [depth, D, D], F32, "ExternalInput")
    w_fi = dt("w_ffn_in", [depth, D, 2 * HID], F32, "ExternalInput")
    w_fo = dt("w_ffn_out", [depth, HID, D], F32, "ExternalInput")
    yT = dt("yT", [D, S], F32, "ExternalOutput")
    xs = dt("xs", [D, S], F32, "Internal")
    QaT = dt("QaT", [512, S], BF16, "Internal")
    KaT = dt("KaT", [256, S], BF16, "Internal")
    QdT = dt("QdT", [512, S], BF16, "Internal")
    KdT = dt("KdT", [512, S], BF16, "Internal")
    OaT = dt("OaT", [512, S], BF16, "Internal")
    ObT = dt("ObT", [512, S], BF16, "Internal")
    H2T = dt("H2T", [D, S], BF16, "Internal")
    Va = dt("Va", [128, NB, 130], BF16, "Internal")
    Vd = dt("Vd", [4, 128, NB, 129], BF16, "Internal")

    chunked = lambda ap2d: ap2d.rearrange("(c p) t -> p c t", p=128)

    with ExitStack() as st:
        P = Prog(nc, st)
        uid = [0]

        def sbuf(stk, name, shape, dtype):
            uid[0] += 1
            return stk.enter_context(nc.sbuf_tensor("%s_u%d" % (name, uid[0]), shape, dtype))
        _pb = [st.enter_context(nc.psum_tensor("pb%d" % i, [128, 512], F32)) for i in range(4)]
        PST = st.enter_context(nc.psum_tensor("pst", [128, 2, 512], F32))
        _pb6 = st.enter_context(nc.psum_tensor("pb6", [128, 512], F32))
        PB = [t[:] for t in _pb] + [PST[:, 0, :], PST[:, 1, :], _pb6[:]]
        PT = st.enter_context(nc.psum_tensor("pt", [128, 1024], BF16))
        par = sbuf(st, "par", [128, NPAR], F32)
        ones_bf = sbuf(st, "ones_bf", [128, 128], BF16)
        blk_bf = sbuf(st, "blk_bf", [128, 128], BF16)
        id_bf = sbuf(st, "id_bf", [128, 128], BF16)
        tri_bf = sbuf(st, "tri_bf", [128, 128], BF16)
        swab = sbuf(st, "swab", [128, 8, 2, 128], F32)
        dtab = sbuf(st, "dtab", [128, 4, NB], F32)
        dist = sbuf(st, "dist", [128, 2, 128], F32)
        esink = sbuf(st, "esink", [128, 32], F32)
        lamt = sbuf(st, "lamt", [128, 16], F32)
        ljunk = sbuf(st, "ljunk", [128, 64], F32)
        nhalf = sbuf(st, "nhalf", [128, 512], F32)

        P.dma("sp", lambda e: e.dma_start(out=par[:], in_=par_in), "par", writes=["par"])
        P.op("pool", lambda e: e.memset(ones_bf[:], 1.0), writes=["ones"])
        P.op("pool", lambda e: e.memset(nhalf[:], -0.5), writes=["nhalf"])
        P.op("pool", lambda e: e.affine_select(out=id_bf[:], in_=ones_bf[:], pattern=[[-1, 128]],
                                               compare_op=ALU.is_equal, fill=0.0, base=0, channel_multiplier=1),
             reads=["ones"], writes=["id"])
        P.op("pool", lambda e: e.affine_select(out=tri_bf[:], in_=ones_bf[:], pattern=[[1, 128]],
                                               compare_op=ALU.is_ge, fill=0.0, base=0, channel_multiplier=-1),
             reads=["ones"], writes=["tri"])
        P.op("pool", lambda e: e.memset(blk_bf[:], 0.0), writes=["blk"])
        P.op("pool", lambda e: e.memset(blk_bf[0:64, 0:64], 1.0), reads=["blk"], writes=["blk"])
        P.op("pool", lambda e: e.memset(blk_bf[64:128, 64:128], 1.0), reads=["blk"], writes=["blk"])
        for kb in range(2):
            P.op("pool", lambda e, kb=kb: e.iota(dist[:, kb, :], pattern=[[1, 128]], base=128 - 128 * kb,
                                                 channel_multiplier=-1, allow_small_or_imprecise_dtypes=True),
                 reads=["dist"], writes=["dist"])
        for h in range(8):
            for kb in range(2):
                P.op("pool", lambda e, h=h, kb=kb: e.tensor_scalar(out=swab[:, h, kb, :], in0=dist[:, kb, :],
                                                                     scalar1=-SLOPES[h], scalar2=0.0, op0=ALU.mult, op1=ALU.add),
                     reads=["dist", "swab"], writes=["swab"])
                if kb == 0:
                    P.op("pool", lambda e, h=h: e.affine_select(out=swab[:, h, 0, :], in_=swab[:, h, 0, :], pattern=[[-1, 128]],
                                                                compare_op=ALU.is_gt, fill=NEGM, base=0, channel_multiplier=1),
                         reads=["swab"], writes=["swab"])
                else:
                    P.op("pool", lambda e, h=h: e.affine_select(out=swab[:, h, 1, :], in_=swab[:, h, 1, :], pattern=[[1, 128]],
                                                                compare_op=ALU.is_ge, fill=NEGM, base=0, channel_multiplier=-1),
                         reads=["swab"], writes=["swab"])
        for h in range(4):
            P.op("pool", lambda e, h=h: e.iota(dtab[:, h, :], pattern=[[-128, NB]], base=128, channel_multiplier=1,
                                               allow_small_or_imprecise_dtypes=True), reads=["dtab"], writes=["dtab"])
            P.op("pool", lambda e, h=h: e.tensor_scalar(out=dtab[:, h, :], in0=dtab[:, h, :], scalar1=SLOPES[8 + h], scalar2=0.0,
                                                        op0=ALU.mult, op1=ALU.add), reads=["dtab"], writes=["dtab"])
        P.op("act", lambda e: e.activation(out=esink[:], in_=par[:, PC_SK:PC_SK + 32], func=AF.Exp), reads=["par"], writes=["esink"])
        for l in range(depth):
            b = PC_LM + 256 * l
            for i in range(2):
                P.op("dve", lambda e, l=l, b=b, i=i: e.scalar_tensor_tensor(
                    out=ljunk[:], in0=par[:, b + 128 * i:b + 128 * i + 64], scalar=1.0, in1=par[:, b + 128 * i + 64:b + 128 * i + 128],
                    op0=ALU.mult, op1=ALU.mult, accum_out=lamt[:, 4 * l + i:4 * l + i + 1]),
                    reads=["par", "ljunk"], writes=["ljunk", "lamt"])
            P.op("act", lambda e, l=l: e.activation(out=lamt[:, 4 * l:4 * l + 2], in_=lamt[:, 4 * l:4 * l + 2], func=AF.Exp),
                 reads=["lamt"], writes=["lamt"])
            P.op("dve", lambda e, l=l: e.scalar_tensor_tensor(out=lamt[:, 4 * l + 2:4 * l + 3], in0=lamt[:, 4 * l:4 * l + 1],
                                                               scalar=float(lam_inits[l]), in1=lamt[:, 4 * l + 1:4 * l + 2],
                                                               op0=ALU.add, op1=ALU.subtract), reads=["lamt"], writes=["lamt"])
            P.op("dve", lambda e, l=l: e.tensor_scalar(out=lamt[:, 4 * l + 3:4 * l + 4], in0=lamt[:, 4 * l + 2:4 * l + 3],
                                                        scalar1=-1.0, scalar2=0.0, op0=ALU.mult, op1=ALU.add),
                 reads=["lamt"], writes=["lamt"])
        P.barrier()
        P.emit()

        def norm_group(xt, sq, rsd, ht, gcol0, keys, pbank, pkey):
            xk, sqk, rsk, hk = keys
            P.op("act", lambda e: e.activation(out=sq[:], in_=xt[:], func=AF.Square), reads=[xk], writes=[sqk])
            P.op("pe", [(lambda e, c=c: e.matmul(pbank[:], lhsT=ones_bf[:], rhs=sq[:, c, :], start=(c == 0), stop=(c == 7)))
                        for c in range(8)], reads=[sqk], writes=[pkey])
            P.op("act", lambda e: e.activation(out=rsd[:], in_=pbank[:], func=AF.Sqrt, scale=1.0 / D, bias=EPS),
                 reads=[pkey], writes=[rsk])
            P.op("dve", lambda e: e.reciprocal(out=rsd[:], in_=rsd[:]), reads=[rsk], writes=[rsk])
            for c in range(8):
                P.op("dve", lambda e, c=c: e.scalar_tensor_tensor(out=ht[:, c, :], in0=xt[:, c, :], scalar=par[:, gcol0 + c:gcol0 + c + 1],
                                                                   in1=rsd[:], op0=ALU.mult, op1=ALU.mult),
                     reads=[xk, rsk], writes=[hk])

        def wload(dst, src, key):
            P.dma("pool", lambda e: e.dma_start(out=dst, in_=src), "w_" + key, writes=[key])

        import os as _os
        PH = _os.environ.get("KPHASES", "A,B1,B2,C1,C2").split(",")
        for l in range(depth):
            x_src = xT_in if l == 0 else xs
            x_dst_final = yT if l == depth - 1 else xs
            lam_init = float(lam_inits[l])

            with ExitStack() as ph:
              if "A" in PH:
                wA = sbuf(ph, "wA", [128, 8, 2432], BF16)
                xT = [sbuf(ph, "xT%d" % i, [128, 8, 512], F32) for i in range(3)]
                sq = sbuf(ph, "sq", [128, 8, 512], BF16)
                rsd = sbuf(ph, "rsd", [128, 512], F32)
                hT = [sbuf(ph, "hT%d" % i, [128, 8, 512], BF16) for i in range(2)]
                sq2 = [sbuf(ph, "sq2_%d" % i, [128, 512], BF16) for i in range(4)]
                rs2 = [sbuf(ph, "rs2_%d" % i, [128, 512], F32) for i in range(2)]
                qn = [sbuf(ph, "qn%d" % i, [128, 512], BF16) for i in range(3)]
                vAs = [sbuf(ph, "vAs%d" % i, [128, 4, 130], BF16) for i in range(2)]
                vDs = [sbuf(ph, "vDs%d" % i, [128, 4, 516], BF16) for i in range(2)]
                wl = chunked(w_in[l])
                for (d0, s0, n, wk) in ((0, 0, 512, "wA0"), (512, 512, 64, "wA1"), (576, 512, 64, "wA1"), (640, 576, 64, "wA1"), (704, 576, 64, "wA1"),
                                        (768, 768, 1024, "wA2"), (1792, 640, 128, "wA3"), (1920, 1792, 512, "wA3")):
                    wload(wA[:, :, d0:d0 + n], wl[:, :, s0:s0 + n], wk)
                for i in range(2):
                    P.op("pool", lambda e, i=i: e.memset(vAs[i][:], 1.0), writes=["vAs%d" % i])
                    P.op("pool", lambda e, i=i: e.memset(vDs[i][:], 1.0), writes=["vDs%d" % i])
                rx, rh, rq2, rqn, rv = Ring("xT", 3), Ring("hT", 2), Ring("q2_", 4), Ring("qn", 3), Ring("vs", 2)
                rpq, rps, rr2 = Ring("pq", 4), Ring("pss", 1), Ring("rs2_", 2)
                xsrc_c = chunked(x_src)
                hslot = {}

                xslot = {}

                def load_x(tg):
                    xi = rx.next()
                    xslot[tg] = xi
                    P.dma("sp", lambda e: e.dma_start(out=xT[xi][:], in_=xsrc_c[:, :, tg * 512:(tg + 1) * 512]),
                          "xT%d" % xi, reads=["xs_%d" % tg], writes=[rx.key(xi)])

                def emit_norm(tg):
                    xi = xslot[tg]
                    hi = rh.next()
                    hslot[tg] = hi
                    norm_group(xT[xi], sq, rsd, hT[hi], PC_NM + 8 * l, (rx.key(xi), "sq", "rsd", rh.key(hi)), PB[6], "pb6")

                pst_ = {}

                def projA(tg, j):
                    hi = hslot[tg]
                    pi = rpq.next()
                    pq, pqk = PB[pi], rpq.key(pi)
                    P.op("pe", [(lambda e, c=c: e.matmul(pq, lhsT=wA[:, c, j * 128:(j + 1) * 128], rhs=hT[hi][:, c, :],
                                                         start=(c == 0), stop=(c == 7))) for c in range(8)],
                         reads=["wA0" if j < 4 else ("wA1" if j < 6 else "wA2"), rh.key(hi)], writes=[pqk])
                    qi = rq2.next()
                    P.op("act", lambda e: e.activation(out=sq2[qi][:], in_=pq, func=AF.Square), reads=[pqk], writes=[rq2.key(qi)])
                    pst_[(tg, j)] = (pq, pqk, qi)

                def restA(tg, j):
                    pq, pqk, qi = pst_.pop((tg, j))
                    si = rps.next()
                    pss, psk = PB[4 + si], rps.key(si)
                    P.op("pe", lambda e: e.matmul(pss, lhsT=blk_bf[:], rhs=sq2[qi][:], start=True, stop=True),
                         reads=[rq2.key(qi)], writes=[psk])
                    ri = rr2.next()
                    rk = rr2.key(ri)
                    P.op("act", lambda e: e.activation(out=rs2[ri][:], in_=pss, func=AF.Sqrt, scale=1.0 / 64, bias=EPS), reads=[psk], writes=[rk])
                    P.op("dve", lambda e: e.reciprocal(out=rs2[ri][:], in_=rs2[ri][:]), reads=[rk], writes=[rk])
                    if j < 4:
                        gc, dst, r0 = PC_QS + 2 * l, QaT, j * 128
                    elif j < 6:
                        gc, dst, r0 = PC_QS + 2 * l + 1, KaT, (j - 4) * 128
                    elif j < 10:
                        gc, dst, r0 = PC_QD + 2 * l, QdT, (j - 6) * 128
                    else:
                        gc, dst, r0 = PC_QD + 2 * l + 1, KdT, (j - 10) * 128
                    ni = rqn.next()
                    P.op("dve", lambda e: e.scalar_tensor_tensor(out=qn[ni][:], in0=pq, scalar=par[:, gc:gc + 1], in1=rs2[ri][:],
                                                                  op0=ALU.mult, op1=ALU.mult), reads=[pqk, rk], writes=[rqn.key(ni)])
                    P.dma("sp", lambda e: e.dma_start(out=dst[r0:r0 + 128, tg * 512:(tg + 1) * 512], in_=qn[ni][:]),
                          "qn%d" % ni, reads=[rqn.key(ni)], writes=["qk_%d" % tg])

                def vblock(tg, hi, hk, vi, tb):
                    P.op("pe", [(lambda e, c=c: e.matmul(PB[6][:, 0:128], lhsT=hT[hi][:, c, tb * 128:(tb + 1) * 128],
                                                         rhs=wA[:, c, 1792:1920], start=(c == 0), stop=(c == 7))) for c in range(8)],
                         reads=["wA3", hk], writes=["pb6"])
                    P.op("act", lambda e: e.activation(
                        out=vAs[vi][:, tb, :].rearrange("p (k d) -> p k d", k=2)[:, :, 0:64],
                        in_=PB[6][:, 0:128].rearrange("p (k d) -> p k d", k=2), func=AF.Copy),
                        reads=["pb6"], writes=["vAs%d" % vi])
                    P.op("pe", [(lambda e, c=c: e.matmul(PB[5], lhsT=hT[hi][:, c, tb * 128:(tb + 1) * 128],
                                                         rhs=wA[:, c, 1920:2432], start=(c == 0), stop=(c == 7))) for c in range(8)],
                         reads=["wA3", hk], writes=["pb5"])
                    P.op("dve", lambda e: e.tensor_copy(
                        out=vDs[vi][:, tb, :].rearrange("p (h d) -> p h d", h=4)[:, :, 0:128],
                        in_=PB[5].rearrange("p (h d) -> p h d", h=4)), reads=["pb5"], writes=["vDs%d" % vi])

                load_x(0)
                if NG > 1:
                    load_x(1)
                emit_norm(0)
                for tg in range(NG):
                    hi = hslot[tg]
                    hk = rh.key(hi)
                    if tg + 2 < NG:
                        load_x(tg + 2)
                    vi = rv.next()
                    projA(tg, 0)
                    for j in range(14):
                        if j + 1 < 14:
                            projA(tg, j + 1)
                        restA(tg, j)
                        if j in (1, 4, 7, 11):
                            vblock(tg, hi, hk, vi, (1, 4, 7, 11).index(j))
                        if j == 9 and tg + 1 < NG:
                            emit_norm(tg + 1)
                    P.dma("sp", lambda e, vi=vi, tg=tg: e.dma_start(out=Va[:, 4 * tg:4 * tg + 4, :], in_=vAs[vi][:]),
                          "vAs%d" % vi, reads=["vAs%d" % vi], writes=["va_%d" % tg])
                    for h in range(4):
                        P.dma("sp", lambda e, vi=vi, tg=tg, h=h: e.dma_start(out=Vd[h, :, 4 * tg:4 * tg + 4, :],
                                                                             in_=vDs[vi][:, :, h * 129:(h + 1) * 129]),
                              "vDs%d" % vi, reads=["vDs%d" % vi], writes=["vd_%d" % tg])
                P.barrier()
                P.emit()

            with ExitStack() as ph:
              if "B1" in PH:
                kA = [sbuf(ph, "kA%d" % i, [128, 2, 640], BF16) for i in range(2)]
                vA = [sbuf(ph, "vA%d" % i, [128, 5, 130], BF16) for i in range(2)]
                qA = [sbuf(ph, "qA%d" % i, [128, 4, 512], BF16) for i in range(2)]
                tmp = [sbuf(ph, "tmp%d" % i, [128, 256], F32) for i in range(3)]
                Es = [sbuf(ph, "Es%d" % i, [128, 256], BF16) for i in range(3)]
                den = [sbuf(ph, "den%d" % i, [128, 1], F32) for i in range(3)]
                oAs = [sbuf(ph, "oAs%d" % i, [128, 128], BF16) for i in range(2)]
                oaT = [sbuf(ph, "oaT%d" % i, [128, 4, 512], BF16) for i in range(2)]
                rin, rt, rE, rden, roA, roT = Ring("swin", 2), Ring("tmp", 3), Ring("Es", 3), Ring("den", 3), Ring("oAs", 2), Ring("oaT", 2)
                rS, rO = Ring("pS", 3), Ring("pO", 2)
                pSb = [PB[0], PB[1], PB[3]]
                pOb = [PB[2], PB[4]]
                KaT_c = KaT.rearrange("(k p) t -> p k t", p=128)
                QaT_c = chunked(QaT)
                OaT_c = chunked(OaT)
                inslot = {}

                def load_in(tg):
                    ii = rin.next()
                    inslot[tg] = ii
                    ik = rin.key(ii)
                    if tg == 0:
                        P.op("pool", lambda e: e.memset(kA[ii][:, :, 0:128], 0.0), writes=[ik])
                        P.op("pool", lambda e: e.memset(vA[ii][:, 0, :], 0.0), reads=[ik], writes=[ik])
                        P.dma("sp", lambda e: e.dma_start(out=kA[ii][:, :, 128:640], in_=KaT_c[:, :, 0:512]), "swin%d" % ii, writes=[ik])
                        P.dma("sp", lambda e: e.dma_start(out=vA[ii][:, 1:5, :], in_=Va[:, 0:4, :]), "swin%d" % ii, writes=[ik])
                    else:
                        P.dma("sp", lambda e: e.dma_start(out=kA[ii][:], in_=KaT_c[:, :, tg * 512 - 128:tg * 512 + 512]), "swin%d" % ii, writes=[ik])
                        P.dma("sp", lambda e: e.dma_start(out=vA[ii][:], in_=Va[:, 4 * tg - 1:4 * tg + 4, :]), "swin%d" % ii, writes=[ik])
                    P.dma("sp", lambda e: e.dma_start(out=qA[ii][:], in_=QaT_c[:, :, tg * 512:(tg + 1) * 512]), "swin%d" % ii, writes=[ik])

                sst = {}

                def scB(tg, it):
                    qc, n, hh = it
                    ii = inslot[tg]
                    ik = rin.key(ii)
                    kv = qc // 2
                    h = 2 * qc + hh
                    pr = slice(64 * hh, 64 * hh + 64)
                    si = rS.next()
                    pS = pSb[si][:, 0:256]
                    psk = rS.key(si)
                    P.op("pe", [(lambda e, kb=kb: e.matmul(pS[:, kb * 128:(kb + 1) * 128], lhsT=kA[ii][pr, kv, (n + kb) * 128:(n + kb + 1) * 128],
                                                           rhs=qA[ii][pr, qc, n * 128:(n + 1) * 128], start=True, stop=True)) for kb in range(2)],
                         reads=[ik], writes=[psk])
                    ti = rt.next()
                    P.op("dve", lambda e: e.scalar_tensor_tensor(out=tmp[ti][:], in0=pS, scalar=0.125, in1=swab[:, h, :, :].rearrange("p a b -> p (a b)"),
                                                                  op0=ALU.mult, op1=ALU.add), reads=[psk], writes=[rt.key(ti)])
                    ei = rE.next()
                    P.op("act", lambda e: e.activation(out=Es[ei][:], in_=tmp[ti][:], func=AF.Exp), reads=[rt.key(ti)], writes=[rE.key(ei)])
                    sst[(tg, it)] = ei

                def pvB(tg, it, oi, oti):
                    qc, n, hh = it
                    ii = inslot[tg]
                    ik = rin.key(ii)
                    kv = qc // 2
                    h = 2 * qc + hh
                    ei = sst.pop((tg, it))
                    pi = rO.next()
                    pO = pOb[pi][:, 0:65]
                    pok = rO.key(pi)
                    P.op("pe", [(lambda e, kb=kb: e.matmul(pO, lhsT=Es[ei][:, kb * 128:(kb + 1) * 128], rhs=vA[ii][:, n + kb, kv * 65:kv * 65 + 65],
                                                           start=(kb == 0), stop=(kb == 1))) for kb in range(2)], reads=[rE.key(ei), ik], writes=[pok])
                    di = rden.next()
                    dk = rden.key(di)
                    P.op("dve", lambda e: e.tensor_scalar(out=den[di][:], in0=pO[:, 64:65], scalar1=esink[:, 8 * l + h:8 * l + h + 1], scalar2=None, op0=ALU.add),
                         reads=[pok], writes=[dk])
                    P.op("dve", lambda e: e.reciprocal(out=den[di][:], in_=den[di][:]), reads=[dk], writes=[dk])
                    P.op("dve", lambda e: e.tensor_scalar(out=oAs[oi][:, 64 * hh:64 * hh + 64], in0=pO[:, 0:64], scalar1=den[di][:, 0:1], scalar2=None, op0=ALU.mult),
                         reads=[pok, dk], writes=[roA.key(oi)])
                    if hh == 1:
                        pendT.append((oi, oti, qc, n))

                pendT = []

                def flushT():
                    while pendT:
                        oi, oti, qc, n = pendT.pop(0)
                        pT = PT[:, 0:128]
                        P.op("pe", lambda e, oi=oi: e.transpose(pT, oAs[oi][:], id_bf[:]), reads=[roA.key(oi)], writes=["pTb1"])
                        P.op("act", lambda e, oti=oti, qc=qc, n=n: e.activation(out=oaT[oti][:, qc, n * 128:(n + 1) * 128], in_=pT, func=AF.Copy),
                             reads=["pTb1"], writes=[roT.key(oti)])

                load_in(0)
                its = [(qc, n, hh) for qc in range(4) for n in range(4) for hh in range(2)]
                for tg in range(NG):
                    if tg + 1 < NG:
                        load_in(tg + 1)
                    oti = roT.next()
                    scB(tg, its[0])
                    scB(tg, its[1])
                    oi = 0
                    for k, it in enumerate(its):
                        if k + 2 < len(its):
                            scB(tg, its[k + 2])
                        if it[2] == 0:
                            oi = roA.next()
                        had = len(pendT) > 0
                        pvB(tg, it, oi, oti)
                        if had:
                            flushT()
                    flushT()
                    P.dma("sp", lambda e, oti=oti, tg=tg: e.dma_start(out=OaT_c[:, :, tg * 512:(tg + 1) * 512], in_=oaT[oti][:]),
                          "oaT%d" % oti, reads=[roT.key(oti)], writes=["oa_%d" % tg])
                P.barrier()
                P.emit()

            with ExitStack() as ph:
              if "B2" in PH:
                Kh = [sbuf(ph, "Kh%d" % i, [128, S], BF16) for i in range(2)]
                Vh = [sbuf(ph, "Vh%d" % i, [128, NB, 129], BF16) for i in range(2)]
                qD = [sbuf(ph, "qD%d" % i, [128, 2, 256], BF16) for i in range(3)]
                Ed = [sbuf(ph, "Ed%d" % i, [128, 2, 256], BF16) for i in range(3)]
                rz = [sbuf(ph, "rz%d" % i, [128, 4], F32) for i in range(2)]
                o0 = [sbuf(ph, "o0_%d" % i, [128, 128], F32) for i in range(2)]
                oo = [sbuf(ph, "oo_%d" % i, [128, 128], F32) for i in range(2)]
                jk = sbuf(ph, "jk", [128, 128], F32)
                ob = [sbuf(ph, "ob_%d" % i, [128, 128], BF16) for i in range(2)]
                obT = [sbuf(ph, "obT%d" % i, [128, 256], BF16) for i in range(2)]
                accS = [[sbuf(ph, "accS%d_%d" % (i, k), [128, 129], F32) for k in range(4)] for i in range(2)]
                rKV, rq, rEd, rfin, robT = Ring("KV", 2), Ring("qD", 3), Ring("Ed", 3), Ring("fin", 2), Ring("obT", 2)
                rST, rT, rAS = Ring("pST", 2), Ring("pTd", 4), Ring("accS", 2)
                acck = ["acc%d" % i for i in range(4)]
                kvslot, qslot, est = {}, {}, {}

                def load_kv(h):
                    ki = rKV.next()
                    kvslot[h] = ki
                    kk = rKV.key(ki)
                    P.dma("sp", lambda e: e.dma_start(out=Kh[ki][:], in_=KdT[h * 128:(h + 1) * 128, :]), "KV%d" % ki, writes=[kk])
                    P.dma("sp", lambda e: e.dma_start(out=Vh[ki][:], in_=Vd[h]), "KV%d" % ki, writes=[kk])

                for i in range(3):
                    P.op("pool", lambda e, i=i: e.memset(qD[i][:], 0.0), writes=["qD%d" % i])

                def load_q(h, g2):
                    qi = rq.next()
                    qslot[(h, g2)] = qi
                    for c in range(2):
                        P.dma("sp", lambda e, c=c: e.dma_start(out=qD[qi][64 * c:64 * c + 64, c, :],
                                                             in_=QdT[h * 128 + 64 * c:h * 128 + 64 * c + 64, g2 * 256:(g2 + 1) * 256]),
                              "qD%d" % qi, writes=[rq.key(qi)])

                def scores(h, g2, m):
                    ki, qi = kvslot[h], qslot[(h, g2)]
                    kk, qk = rKV.key(ki), rq.key(qi)
                    c0 = 0 if m <= 2 * g2 else 128
                    dd = 2 * g2 - m
                    sti = rST.next()
                    stk = rST.key(sti)
                    pst = PB[4 + sti].rearrange("p (c q) -> p c q", c=2)
                    if c0 == 0:
                        P.op("pe", lambda e: e.matmul(PB[4 + sti], lhsT=Kh[ki][:, m * 128:(m + 1) * 128],
                                                      rhs=qD[qi][:].rearrange("p c q -> p (c q)"), start=True, stop=True),
                             reads=[kk, qk], writes=[stk])
                    else:
                        P.op("pe", [(lambda e, c=c: e.matmul(pst[:, c, c0:256], lhsT=Kh[ki][:, m * 128:(m + 1) * 128],
                                                             rhs=qD[qi][:, c, c0:256], start=True, stop=True)) for c in range(2)],
                             reads=[kk, qk], writes=[stk])
                    ei = rEd.next()
                    ek = rEd.key(ei)
                    P.op("act", lambda e: e.activation(
                        out=Ed[ei][:, :, c0:256], in_=pst[:, :, c0:256], func=AF.Exp,
                        scale=0.125, bias=dtab[:, h, dd + 1:dd + 2]), reads=[stk], writes=[ek])
                    if m >= 2 * g2:
                        for c in range(2):
                            P.op("pool", lambda e, c=c: e.tensor_tensor(
                                out=Ed[ei][:, c, c0:c0 + 128], in0=Ed[ei][:, c, c0:c0 + 128], in1=tri_bf[:], op=ALU.mult),
                                reads=[ek], writes=[ek])
                    est[(h, g2, m)] = (ei, c0 // 128)

                def pv(h, g2, m):
                    ei, nq0 = est.pop((h, g2, m))
                    ki = kvslot[h]
                    fns, wr = [], []
                    for c in range(2):
                        for nn in range(nq0, 2):
                            fns.append(lambda e, c=c, nn=nn: e.matmul(
                                PB[2 * c + nn][:, 0:129], lhsT=Ed[ei][:, c, nn * 128:(nn + 1) * 128], rhs=Vh[ki][:, m, :],
                                start=(m == 0), stop=(m == 2 * g2 + nn)))
                            wr.append(acck[2 * c + nn])
                    P.op("pe", fns, reads=[rEd.key(ei), rKV.key(ki)], writes=wr)

                pending = []

                def finalize(h, g2):
                    finalize2()
                    si = rAS.next()
                    sk = rAS.key(si)
                    A = accS[si]
                    for k in range(4):
                        P.op("dve", lambda e, k=k: e.tensor_copy(out=A[k][:], in_=PB[k][:, 0:129]), reads=[acck[k]], writes=[sk + "_%d" % k])
                    oi = robT.next()
                    obk = robT.key(oi)
                    fis = []
                    for nn in range(2):
                        fi = rfin.next()
                        fk = rfin.key(fi)
                        a0, a1 = A[nn], A[2 + nn]
                        k0, k1 = sk + "_%d" % nn, sk + "_%d" % (2 + nn)
                        P.op("dve", lambda e, fi=fi, a0=a0: e.reciprocal(out=rz[fi][:, 0:1], in_=a0[:, 128:129]), reads=[k0], writes=[fk])
                        P.op("dve", lambda e, fi=fi, a1=a1: e.reciprocal(out=rz[fi][:, 1:2], in_=a1[:, 128:129]), reads=[k1, fk], writes=[fk])
                        P.op("dve", lambda e, fi=fi: e.tensor_tensor(out=rz[fi][:, 2:3], in0=rz[fi][:, 1:2], in1=lamt[:, 4 * l + 3:4 * l + 4], op=ALU.mult),
                             reads=[fk], writes=[fk])
                        P.op("dve", lambda e, fi=fi, a0=a0: e.tensor_scalar(out=o0[fi][:], in0=a0[:, 0:128], scalar1=rz[fi][:, 0:1], scalar2=None, op0=ALU.mult),
                             reads=[k0, fk], writes=[fk])
                        P.op("dve", lambda e, fi=fi, a1=a1: e.scalar_tensor_tensor(out=oo[fi][:], in0=a1[:, 0:128], scalar=rz[fi][:, 2:3], in1=o0[fi][:],
                                                                                   op0=ALU.mult, op1=ALU.add), reads=[k1, fk], writes=[fk])
                        P.op("dve", lambda e, fi=fi: e.scalar_tensor_tensor(out=jk[:], in0=oo[fi][:], scalar=1.0, in1=oo[fi][:], op0=ALU.mult, op1=ALU.mult,
                                                                            accum_out=rz[fi][:, 3:4]), reads=[fk, "jk"], writes=[fk, "jk"])
                        P.op("dve", lambda e, fi=fi: e.tensor_scalar(out=rz[fi][:, 3:4], in0=rz[fi][:, 3:4], scalar1=1.0 / 128, scalar2=EPS, op0=ALU.mult, op1=ALU.add),
                             reads=[fk], writes=[fk])
                        P.op("pool", lambda e, fi=fi: e.tensor_tensor(out=rz[fi][:, 3:4], in0=rz[fi][:, 3:4], in1=nhalf[:, 0:1], op=ALU.pow),
                             reads=[fk], writes=[fk])
                        P.op("dve", lambda e, fi=fi: e.tensor_scalar(out=ob[fi][:], in0=oo[fi][:], scalar1=rz[fi][:, 3:4], scalar2=None, op0=ALU.mult),
                             reads=[fk], writes=[fk])
                        fis.append((fi, fk))
                    pending.append((h, g2, oi, obk, fis))

                def finalize2():
                    while pending:
                        h, g2, oi, obk, fis = pending.pop(0)
                        for nn, (fi, fk) in enumerate(fis):
                            ti2 = 0
                            pT = PT[:, ti2 * 128:(ti2 + 1) * 128]
                            P.op("pe", lambda e, pT=pT, fi=fi: e.transpose(pT, ob[fi][:], id_bf[:]), reads=[fk], writes=[rT.key(ti2)])
                            P.op("dve", lambda e, pT=pT, nn=nn, oi=oi: e.tensor_scalar(
                                out=obT[oi][:, nn * 128:(nn + 1) * 128], in0=pT, scalar1=par[:, PC_SL + l:PC_SL + l + 1], scalar2=1.0 - lam_init,
                                op0=ALU.mult, op1=ALU.mult), reads=[rT.key(ti2)], writes=[obk])
                        P.dma("sp", lambda e, h=h, g2=g2, oi=oi: e.dma_start(out=ObT[h * 128:(h + 1) * 128, g2 * 256:(g2 + 1) * 256], in_=obT[oi][:]),
                              "obT%d" % oi, reads=[obk], writes=["ob_%d_%d" % (h, g2)])

                items = [(h, g2, m) for h in range(4) for g2 in range(NG2) for m in range(2 * g2 + 2)]
                groups = [(h, g2) for h in range(4) for g2 in range(NG2)]
                gidx = {g: i for i, g in enumerate(groups)}

                def prep(i):
                    h, g2, m = items[i]
                    if m == 0:
                        gi = gidx[(h, g2)]
                        if gi == 0:
                            load_kv(0)
                            load_q(0, 0)
                        if g2 == 1 and h + 1 < 4:
                            load_kv(h + 1)
                        if gi + 1 < len(groups):
                            load_q(*groups[gi + 1])
                    scores(h, g2, m)

                prep(0)
                since = 0
                for i in range(len(items)):
                    if i + 1 < len(items):
                        prep(i + 1)
                    h, g2, m = items[i]
                    pv(h, g2, m)
                    since += 1
                    if pending and since >= 6:
                        finalize2()
                    if m == 2 * g2 + 1:
                        finalize(h, g2)
                        since = 0
                finalize2()
                P.barrier()
                P.emit()

            with ExitStack() as ph:
              if "C1" in PH:
                wG = sbuf(ph, "wG", [128, 8, 2048], BF16)
                wB = sbuf(ph, "wB", [128, 2, 4, 1024], BF16)
                wO = sbuf(ph, "wO", [128, 8, 1024], BF16)
                xT = [sbuf(ph, "xT%d" % i, [128, 8, 512], F32) for i in range(2)]
                sq = sbuf(ph, "sq", [128, 8, 512], BF16)
                rsd = sbuf(ph, "rsd", [128, 512], F32)
                hT = [sbuf(ph, "hT%d" % i, [128, 8, 512], BF16) for i in range(2)]
                gT = sbuf(ph, "gT", [128, 16, 512], BF16)
                oab = [sbuf(ph, "oab%d" % i, [128, 8, 512], BF16) for i in range(2)]
                m1 = [sbuf(ph, "m1_%d" % i, [128, 512], F32) for i in range(2)]
                m2 = [sbuf(ph, "m2_%d" % i, [128, 512], F32) for i in range(2)]
                mT = sbuf(ph, "mT", [128, 8, 512], BF16)
                wl = chunked(w_in[l])
                for q4 in range(4):
                    wload(wG[:, :, q4 * 512:(q4 + 1) * 512], wl[:, :, 2304 + q4 * 512:2304 + (q4 + 1) * 512], "wG%d" % q4)
                for n in range(2):
                    wload(wB[:, n, :, :], w_br[l, n].rearrange("(k p) d -> p k d", p=128), "wB")
                wol = chunked(w_o[l])
                for q2 in range(2):
                    wload(wO[:, :, q2 * 512:(q2 + 1) * 512], wol[:, :, q2 * 512:(q2 + 1) * 512], "wO")
                rx, roab, rm = Ring("xT", 2), Ring("oab", 2), Ring("m", 2)
                rpq = Ring("pq", 2)
                rpa = Ring("pa", 2)
                xsrc_c = chunked(x_src)
                xs_c = chunked(xs)
                OaT_c, ObT_c = chunked(OaT), chunked(ObT)
                c1slot = {}

                def pro_c1(tg):
                    xi = rx.next()
                    xk = rx.key(xi)
                    P.dma("sp", lambda e: e.dma_start(out=xT[xi][:], in_=xsrc_c[:, :, tg * 512:(tg + 1) * 512]),
                          "xT%d" % xi, reads=["xs_%d" % tg], writes=[xk])
                    ai = roab.next()
                    ak = roab.key(ai)
                    P.dma("sp", lambda e: e.dma_start(out=oab[ai][:, 0:4, :], in_=OaT_c[:, :, tg * 512:(tg + 1) * 512]), "oab%d" % ai, writes=[ak])
                    P.dma("sp", lambda e: e.dma_start(out=oab[ai][:, 4:8, :], in_=ObT_c[:, :, tg * 512:(tg + 1) * 512]), "oab%d" % ai, writes=[ak])
                    c1slot[tg] = (xi, ai)

                rhc = Ring("hTc", 2)
                hcs = {}

                def norm_c1(tg):
                    xi_, _ = c1slot[tg]
                    hi_ = rhc.next()
                    hcs[tg] = hi_
                    norm_group(xT[xi_], sq, rsd, hT[hi_], PC_NM + 8 * l, (rx.key(xi_), "sq", "rsd", rhc.key(hi_)), PB[6], "pb6")

                pro_c1(0)
                norm_c1(0)
                for tg in range(NG):
                    xi, ai = c1slot[tg]
                    xk, ak = rx.key(xi), roab.key(ai)
                    hi = hcs[tg]
                    hk = rhc.key(hi)
                    if tg + 1 < NG:
                        pro_c1(tg + 1)
                    for j in range(16):
                        if j == 10 and tg + 1 < NG:
                            norm_c1(tg + 1)
                        pi = rpq.next()
                        pq, pqk = PB[pi], rpq.key(pi)
                        P.op("pe", [(lambda e, c=c, j=j, pq=pq, hi=hi: e.matmul(pq[:], lhsT=wG[:, c, j * 128:(j + 1) * 128], rhs=hT[hi][:, c, :],
                                                                                  start=(c == 0), stop=(c == 7))) for c in range(8)],
                             reads=["wG%d" % (j // 4), hk], writes=[pqk])
                        P.op("act", lambda e, j=j, pq=pq: e.activation(out=gT[:, j, :], in_=pq[:], func=AF.Sigmoid,
                                                                       bias=par[:, PC_BG + 16 * l + j:PC_BG + 16 * l + j + 1]),
                             reads=[pqk], writes=["gT"])
                    for j in range(8):
                        pi = rpa.next()
                        pa, pb_ = PB[2 + 2 * pi], PB[3 + 2 * pi]
                        pak = rpa.key(pi)
                        P.op("pe", [(lambda e, kc=kc, j=j, pa=pa, ai=ai: e.matmul(pa[:], lhsT=wB[:, 0, kc, j * 128:(j + 1) * 128], rhs=oab[ai][:, kc, :],
                                                                                     start=(kc == 0), stop=(kc == 3))) for kc in range(4)] +
                                   [(lambda e, kc=kc, j=j, pb_=pb_, ai=ai: e.matmul(pb_[:], lhsT=wB[:, 1, kc, j * 128:(j + 1) * 128], rhs=oab[ai][:, 4 + kc, :],
                                                                                       start=(kc == 0), stop=(kc == 3))) for kc in range(4)],
                             reads=["wB", ak], writes=[pak])
                        mi = rm.next()
                        mk = rm.key(mi)
                        P.op("dve", lambda e, mi=mi, pa=pa, j=j: e.tensor_tensor(out=m1[mi][:], in0=pa[:], in1=gT[:, j, :], op=ALU.mult),
                             reads=[pak, "gT"], writes=[mk])
                        P.op("dve", lambda e, mi=mi, pb_=pb_, j=j: e.tensor_tensor(out=m2[mi][:], in0=pb_[:], in1=gT[:, 8 + j, :], op=ALU.mult),
                             reads=[pak, "gT", mk], writes=[mk])
                        P.op("pool", lambda e, mi=mi, j=j: e.tensor_tensor(out=mT[:, j, :], in0=m1[mi][:], in1=m2[mi][:], op=ALU.add),
                             reads=[mk], writes=["mT"])
                    for j in range(8):
                        pi = rpq.next()
                        pq, pqk = PB[pi], rpq.key(pi)
                        P.op("pe", [(lambda e, c=c, j=j, pq=pq: e.matmul(pq[:], lhsT=wO[:, c, j * 128:(j + 1) * 128], rhs=mT[:, c, :],
                                                                           start=(c == 0), stop=(c == 7))) for c in range(8)],
                             reads=["wO", "mT"], writes=[pqk])
                        P.op("dve", lambda e, j=j, pq=pq, xi=xi: e.tensor_tensor(out=xT[xi][:, j, :], in0=xT[xi][:, j, :], in1=pq[:], op=ALU.add),
                             reads=[pqk, xk], writes=[xk])
                    P.dma("sp", lambda e, xi=xi, tg=tg: e.dma_start(out=xs_c[:, :, tg * 512:(tg + 1) * 512], in_=xT[xi][:]),
                          "xo%d" % xi, reads=[xk], writes=["xs_%d" % tg])
                P.barrier()
                P.emit()

            for half in range(2):
                with ExitStack() as ph:
                  if "C2" in PH:
                    wI = sbuf(ph, "wI", [128, 8, 2, 1408], BF16)
                    wF = sbuf(ph, "wF", [128, 11, 1024], BF16)
                    xT = [sbuf(ph, "xT%d" % i, [128, 8, 512], F32) for i in range(2)]
                    sq = sbuf(ph, "sq", [128, 8, 512], BF16)
                    rsd = sbuf(ph, "rsd", [128, 512], F32)
                    h2 = [sbuf(ph, "h2_%d" % i, [128, 8, 512], BF16) for i in range(2)]
                    sil = [sbuf(ph, "sil%d" % i, [128, 512], F32) for i in range(2)]
                    aT = sbuf(ph, "aT", [128, 11, 512], BF16)
                    wil = chunked(w_fi[l])
                    for gu in range(2):
                        wload(wI[:, :, gu, :], wil[:, :, gu * HID + half * 1408:gu * HID + (half + 1) * 1408], "wI")
                    wfl = w_fo[l, half * 1408:(half + 1) * 1408, :].rearrange("(k p) d -> p k d", p=128)
                    for q2 in range(2):
                        wload(wF[:, :, q2 * 512:(q2 + 1) * 512], wfl[:, :, q2 * 512:(q2 + 1) * 512], "wF")
                    rx, rh2, rsl = Ring("xT", 2), Ring("h2_", 2), Ring("sil", 2)
                    rpg, rpo = Ring("pg", 2), Ring("po", 2)
                    xs_c = chunked(xs)
                    H2_c = chunked(H2T)
                    dst_c = chunked(x_dst_final if half == 1 else xs)
                    c2slot = {}

                    def pro_c2(tg):
                        xi = rx.next()
                        xk = rx.key(xi)
                        P.dma("sp", lambda e: e.dma_start(out=xT[xi][:], in_=xs_c[:, :, tg * 512:(tg + 1) * 512]),
                              "xT%d" % xi, reads=["xs_%d" % tg], writes=[xk])
                        hi = rh2.next()
                        hk = rh2.key(hi)
                        if half == 1:
                            P.dma("sp", lambda e: e.dma_start(out=h2[hi][:], in_=H2_c[:, :, tg * 512:(tg + 1) * 512]),
                                  "h2i%d" % hi, reads=["h2_%d" % tg], writes=[hk])
                        c2slot[tg] = (xi, hi)

                    def norm_c2(tg):
                        xi_, hi_ = c2slot[tg]
                        norm_group(xT[xi_], sq, rsd, h2[hi_], PC_NF + 8 * l, (rx.key(xi_), "sq", "rsd", rh2.key(hi_)), PB[6], "pb6")
                        P.dma("sp", lambda e: e.dma_start(out=H2_c[:, :, tg * 512:(tg + 1) * 512], in_=h2[hi_][:]),
                              "h2o%d" % hi_, reads=[rh2.key(hi_)], writes=["h2_%d" % tg])

                    pro_c2(0)
                    for tg in range(NG):
                        xi, hi = c2slot[tg]
                        xk, hk = rx.key(xi), rh2.key(hi)
                        if tg + 1 < NG:
                            pro_c2(tg + 1)
                        if half == 0 and tg == 0:
                            norm_c2(0)
                        for jj in range(11):
                            if half == 0 and jj == 6 and tg + 1 < NG:
                                norm_c2(tg + 1)
                            pi = rpg.next()
                            pg, pu = PB[2 * pi], PB[2 * pi + 1]
                            pgk = rpg.key(pi)
                            P.op("pe", [(lambda e, c=c, jj=jj, pg=pg, hi=hi: e.matmul(pg[:], lhsT=wI[:, c, 0, jj * 128:(jj + 1) * 128], rhs=h2[hi][:, c, :],
                                                                                         start=(c == 0), stop=(c == 7))) for c in range(8)] +
                                       [(lambda e, c=c, jj=jj, pu=pu, hi=hi: e.matmul(pu[:], lhsT=wI[:, c, 1, jj * 128:(jj + 1) * 128], rhs=h2[hi][:, c, :],
                                                                                         start=(c == 0), stop=(c == 7))) for c in range(8)],
                                 reads=["wI", hk], writes=[pgk])
                            si = rsl.next()
                            P.op("act", lambda e, si=si, pg=pg: e.activation(out=sil[si][:], in_=pg[:], func=AF.Silu), reads=[pgk], writes=[rsl.key(si)])
                            P.op("dve", lambda e, si=si, pu=pu, jj=jj: e.tensor_tensor(out=aT[:, jj, :], in0=sil[si][:], in1=pu[:], op=ALU.mult),
                                 reads=[pgk, rsl.key(si)], writes=["aT"])
                        for j in range(8):
                            pi = rpo.next()
                            po, pok = PB[4 + pi], rpo.key(pi)
                            P.op("pe", [(lambda e, jj=jj, j=j, po=po: e.matmul(po[:], lhsT=wF[:, jj, j * 128:(j + 1) * 128], rhs=aT[:, jj, :],
                                                                                 start=(jj == 0), stop=(jj == 10))) for jj in range(11)],
                                 reads=["wF", "aT"], writes=[pok])
                            P.op("dve", lambda e, j=j, po=po, xi=xi: e.tensor_tensor(out=xT[xi][:, j, :], in0=xT[xi][:, j, :], in1=po[:], op=ALU.add),
                                 reads=[pok, xk], writes=[xk])
                        P.dma("sp", lambda e, xi=xi, tg=tg: e.dma_start(out=dst_c[:, :, tg * 512:(tg + 1) * 512], in_=xT[xi][:]),
                              "xo%d" % xi, reads=[xk], writes=["xs_%d" % tg])
                    P.barrier()
                    P.emit()
        build_nc.n_ins = P.n_ins
    return nc


def pack_params(b_gate, norm_mix, norm_ffn, qk_norm_swa, qk_norm_diff, attn_sinks, diff_lambda, diff_subln):
    depth = b_gate.shape[0]
    par = np.zeros((128, NPAR), np.float32)
    f = lambda a: np.asarray(a, np.float32)
    par[:, PC_BG:PC_BG + 16 * depth] = f(b_gate).reshape(depth, 16, 128).transpose(2, 0, 1).reshape(128, 16 * depth)
    par[:, PC_NM:PC_NM + 8 * depth] = f(norm_mix).reshape(depth, 8, 128).transpose(2, 0, 1).reshape(128, 8 * depth)
    par[:, PC_NF:PC_NF + 8 * depth] = f(norm_ffn).reshape(depth, 8, 128).transpose(2, 0, 1).reshape(128, 8 * depth)
    qs = f(qk_norm_swa).reshape(depth * 2, 64).T
    par[:, PC_QS:PC_QS + 2 * depth] = np.concatenate([qs, qs], axis=0)
    qd = f(qk_norm_diff).reshape(depth * 2, 64).T
    par[:, PC_QD:PC_QD + 2 * depth] = np.concatenate([qd, qd], axis=0)
    par[:, PC_SK:PC_SK + 8 * depth] = np.broadcast_to(f(attn_sinks).reshape(1, 8 * depth), (128, 8 * depth))
    par[:, PC_SL:PC_SL + depth] = f(diff_subln).T
    par[:, PC_LM:PC_LM + 256 * depth] = np.broadcast_to(f(diff_lambda).reshape(1, 256 * depth), (128, 256 * depth))
    return par


def run_model(x, w_in, b_gate, w_branch, w_o, norm_mix, norm_ffn, qk_norm_swa, qk_norm_diff,
              attn_sinks, diff_lambda, diff_subln, w_ffn_in, w_ffn_out, runner=None):
    x = np.asarray(x, np.float32)
    B, S, _ = x.shape
    depth = np.asarray(w_in).shape[0]
    lam_inits = [0.8 - 0.6 * math.exp(-0.3 * l) for l in range(depth)]
    nc = build_nc(S, depth, lam_inits)
    par = pack_params(b_gate, norm_mix, norm_ffn, qk_norm_swa, qk_norm_diff, attn_sinks, diff_lambda, diff_subln)
    ws = {"w_in": np.ascontiguousarray(w_in, np.float32), "w_branch": np.ascontiguousarray(w_branch, np.float32),
          "w_o": np.ascontiguousarray(w_o, np.float32), "w_ffn_in": np.ascontiguousarray(w_ffn_in, np.float32),
          "w_ffn_out": np.ascontiguousarray(w_ffn_out, np.float32), "par": par}
    n_cores = 8
    in_maps = []
    for c in range(n_cores):
        b = (c * B) // n_cores
        m = {"xT": np.ascontiguousarray(x[b].T)}
        m.update(ws)
        in_maps.append(m)
    if runner is None:
        res = run_bass_kernel_spmd(nc, in_maps, core_ids=list(range(n_cores))).results
    else:
        res = runner(nc, in_maps)
    out = np.empty((B, S, D), np.float32)
    per = n_cores // B
    for b in range(B):
        out[b] = res[b * per]["yT"].T
    return out


def kernel(**inputs):
    return run_model(**inputs)
```

```python
import math
from contextlib import ExitStack
import numpy as np
import concourse.bass as bass
import concourse.mybir as mybir
from concourse.bass_utils import run_bass_kernel_spmd

F32 = mybir.dt.float32
BF16 = mybir.dt.bfloat16
AF = mybir.ActivationFunctionType
ALU = mybir.AluOpType

D = 1024
NCH = 8
HID = 2816
IN_COLS = 4352
EPS = 1e-6
NEGM = -30000.0
SLOPES = [2.0 ** (-8.0 * i / 12.0) for i in range(1, 13)]

PC_BG = 0
PC_NM = 64
PC_NF = 96
PC_QS = 128
PC_QD = 136
PC_SK = 144
PC_SL = 176
PC_LM = 180
NPAR = 180 + 1024


class Prog:
    COMPUTE = ("pe", "act", "dve", "pool")

    def __init__(self, nc, stack):
        self.nc = nc
        self.stack = stack
        self.q = {k: [] for k in ("pe", "act", "dve", "pool", "sp")}
        self.sem = {}
        self.cnt = {}
        for e in self.COMPUTE:
            self.sem[e] = stack.enter_context(nc.semaphore("prog_" + e))
            self.cnt[e] = 0
        self.dsem = {}
        self.dcnt = {}
        self.last_write = {}
        self.readers = {}
        self.seen = {k: {} for k in self.q}
        self.n_ins = 0

    def _deps(self, reads, writes):
        deps = []
        for r in reads:
            t = self.last_write.get(r)
            if t is not None:
                deps.append(t)
        for w in writes:
            t = self.last_write.get(w)
            if t is not None:
                deps.append(t)
            deps.extend(self.readers.get(w, ()))
        return deps

    def _emit_waits(self, eng, deps):
        seen = self.seen[eng]
        need = {}
        for (owner, sem, val) in deps:
            if owner == "pe" and eng == "pe":
                continue
            key = id(sem)
            if seen.get(key, 0) >= val:
                continue
            if key not in need or need[key][1] < val:
                need[key] = (sem, val)
        for key, (sem, val) in need.items():
            seen[key] = val
            self.q[eng].append(("wait", sem, val))

    def _commit(self, token, reads, writes):
        for w in writes:
            self.last_write[w] = token
            self.readers[w] = []
        for r in reads:
            self.readers.setdefault(r, []).append(token)

    def op(self, eng, fn, reads=(), writes=()):
        self._emit_waits(eng, self._deps(reads, writes))
        self.cnt[eng] += 1
        token = (eng, self.sem[eng], self.cnt[eng])
        fns = fn if isinstance(fn, (list, tuple)) else [fn]
        for f in fns[:-1]:
            self.q[eng].append(("ins", f, None))
        self.q[eng].append(("ins", fns[-1], self.sem[eng]))
        self.n_ins += len(fns)
        self._commit(token, reads, writes)

    def dma(self, queue, fn, semkey, reads=(), writes=()):
        self._emit_waits(queue, self._deps(reads, writes))
        if semkey not in self.dsem:
            self.dsem[semkey] = self.stack.enter_context(self.nc.semaphore("d_" + str(semkey)))
            self.dcnt[semkey] = 0
        self.dcnt[semkey] += 16
        sem = self.dsem[semkey]
        token = ("dma", sem, self.dcnt[semkey])
        self.q[queue].append(("dma", fn, sem))
        self.n_ins += 1
        self._commit(token, reads, writes)

    def barrier(self):
        toks = [(e, self.sem[e], self.cnt[e]) for e in self.COMPUTE if self.cnt[e] > 0]
        toks += [("dma", self.dsem[k], self.dcnt[k]) for k in self.dsem]
        for eng in self.q:
            self._emit_waits(eng, [t for t in toks if not (t[0] == eng and eng != "pe" and False)])
        self.last_write = {}
        self.readers = {}

    def emit(self):
        nc = self.nc
        q = self.q

        def run(engine, items):
            for it in items:
                if it[0] == "wait":
                    engine.wait_ge(it[1], it[2])
                elif it[0] == "ins":
                    ins = it[1](engine)
                    if it[2] is not None:
                        ins.then_inc(it[2], 1)
                else:
                    it[1](engine).then_inc(it[2], 16)

        with nc.Block() as block:
            @block.tensor
            def _(e):
                run(e, q["pe"])

            @block.scalar
            def _(e):
                run(e, q["act"])

            @block.vector
            def _(e):
                run(e, q["dve"])

            @block.gpsimd
            def _(e):
                run(e, q["pool"])

            @block.sync
            def _(e):
                run(e, q["sp"])
        self.q = {k: [] for k in q}


class Ring:
    def __init__(self, name, n):
        self.name, self.n, self.i = name, n, -1

    def next(self):
        self.i = (self.i + 1) % self.n
        return self.i

    def key(self, i):
        return "%s%d" % (self.name, i)


def build_nc(S, depth, lam_inits):
    NG = S // 512
    NB = S // 128
    NG2 = S // 256
    nc = bass.Bass("TRN2", target_bir_lowering=False)
    dt = lambda name, shape, dtype, kind: nc.dram_tensor(name, shape, dtype, kind=kind).ap()
    xT_in = dt("xT", [D, S], F32, "ExternalInput")
    par_in = dt("par", [128, NPAR], F32, "ExternalInput")
    w_in = dt("w_in", [depth, D, IN_COLS], F32, "ExternalInput")
    w_br = dt("w_branch", [depth, 2, 512, D], F32, "ExternalInput")
    w_o = dt("w_o", [depth, D, D], F32, "ExternalInput")
    w_fi = dt("w_ffn_in", [depth, D, 2 * HID], F32, "ExternalInput")
    w_fo = dt("w_ffn_out", [depth, HID, D], F32, "ExternalInput")
    yT = dt("yT", [D, S], F32, "ExternalOutput")
    xs = dt("xs", [D, S], F32, "Internal")
    QaT = dt("QaT", [512, S], BF16, "Internal")
    KaT = dt("KaT", [256, S], BF16, "Internal")
    QdT = dt("QdT", [512, S], BF16, "Internal")
    KdT = dt("KdT", [512, S], BF16, "Internal")
    OaT = dt("OaT", [512, S], BF16, "Internal")
    ObT = dt("ObT", [512, S], BF16, "Internal")
    H2T = dt("H2T", [D, S], BF16, "Internal")
    Va = dt("Va", [128, NB, 130], BF16, "Internal")
    Vd = dt("Vd", [4, 128, NB, 129], BF16, "Internal")

    chunked = lambda ap2d: ap2d.rearrange("(c p) t -> p c t", p=128)

    with ExitStack() as st:
        P = Prog(nc, st)
        uid = [0]

        def sbuf(stk, name, shape, dtype):
            uid[0] += 1
            return stk.enter_context(nc.sbuf_tensor("%s_u%d" % (name, uid[0]), shape, dtype))
        _pb = [st.enter_context(nc.psum_tensor("pb%d" % i, [128, 512], F32)) for i in range(4)]
        PST = st.enter_context(nc.psum_tensor("pst", [128, 2, 512], F32))
        _pb6 = st.enter_context(nc.psum_tensor("pb6", [128, 512], F32))
        PB = [t[:] for t in _pb] + [PST[:, 0, :], PST[:, 1, :], _pb6[:]]
        PT = st.enter_context(nc.psum_tensor("pt", [128, 1024], BF16))
        par = sbuf(st, "par", [128, NPAR], F32)
        ones_bf = sbuf(st, "ones_bf", [128, 128], BF16)
        blk_bf = sbuf(st, "blk_bf", [128, 128], BF16)
        id_bf = sbuf(st, "id_bf", [128, 128], BF16)
        tri_bf = sbuf(st, "tri_bf", [128, 128], BF16)
        swab = sbuf(st, "swab", [128, 8, 2, 128], F32)
        dtab = sbuf(st, "dtab", [128, 4, NB], F32)
        dist = sbuf(st, "dist", [128, 2, 128], F32)
        esink = sbuf(st, "esink", [128, 32], F32)
        lamt = sbuf(st, "lamt", [128, 16], F32)
        ljunk = sbuf(st, "ljunk", [128, 64], F32)
        nhalf = sbuf(st, "nhalf", [128, 512], F32)

        P.dma("sp", lambda e: e.dma_start(out=par[:], in_=par_in), "par", writes=["par"])
        P.op("pool", lambda e: e.memset(ones_bf[:], 1.0), writes=["ones"])
        P.op("pool", lambda e: e.memset(nhalf[:], -0.5), writes=["nhalf"])
        P.op("pool", lambda e: e.affine_select(out=id_bf[:], in_=ones_bf[:], pattern=[[-1, 128]],
                                               compare_op=ALU.is_equal, fill=0.0, base=0, channel_multiplier=1),
             reads=["ones"], writes=["id"])
        P.op("pool", lambda e: e.affine_select(out=tri_bf[:], in_=ones_bf[:], pattern=[[1, 128]],
                                               compare_op=ALU.is_ge, fill=0.0, base=0, channel_multiplier=-1),
             reads=["ones"], writes=["tri"])
        P.op("pool", lambda e: e.memset(blk_bf[:], 0.0), writes=["blk"])
        P.op("pool", lambda e: e.memset(blk_bf[0:64, 0:64], 1.0), reads=["blk"], writes=["blk"])
        P.op("pool", lambda e: e.memset(blk_bf[64:128, 64:128], 1.0), reads=["blk"], writes=["blk"])
        for kb in range(2):
            P.op("pool", lambda e, kb=kb: e.iota(dist[:, kb, :], pattern=[[1, 128]], base=128 - 128 * kb,
                                                 channel_multiplier=-1, allow_small_or_imprecise_dtypes=True),
                 reads=["dist"], writes=["dist"])
        for h in range(8):
            for kb in range(2):
                P.op("pool", lambda e, h=h, kb=kb: e.tensor_scalar(out=swab[:, h, kb, :], in0=dist[:, kb, :],
                                                                     scalar1=-SLOPES[h], scalar2=0.0, op0=ALU.mult, op1=ALU.add),
                     reads=["dist", "swab"], writes=["swab"])
                if kb == 0:
                    P.op("pool", lambda e, h=h: e.affine_select(out=swab[:, h, 0, :], in_=swab[:, h, 0, :], pattern=[[-1, 128]],
                                                                compare_op=ALU.is_gt, fill=NEGM, base=0, channel_multiplier=1),
                         reads=["swab"], writes=["swab"])
                else:
                    P.op("pool", lambda e, h=h: e.affine_select(out=swab[:, h, 1, :], in_=swab[:, h, 1, :], pattern=[[1, 128]],
                                                                compare_op=ALU.is_ge, fill=NEGM, base=0, channel_multiplier=-1),
                         reads=["swab"], writes=["swab"])
        for h in range(4):
            P.op("pool", lambda e, h=h: e.iota(dtab[:, h, :], pattern=[[-128, NB]], base=128, channel_multiplier=1,
                                               allow_small_or_imprecise_dtypes=True), reads=["dtab"], writes=["dtab"])
            P.op("pool", lambda e, h=h: e.tensor_scalar(out=dtab[:, h, :], in0=dtab[:, h, :], scalar1=SLOPES[8 + h], scalar2=0.0,
                                                        op0=ALU.mult, op1=ALU.add), reads=["dtab"], writes=["dtab"])
        P.op("act", lambda e: e.activation(out=esink[:], in_=par[:, PC_SK:PC_SK + 32], func=AF.Exp), reads=["par"], writes=["esink"])
        for l in range(depth):
            b = PC_LM + 256 * l
            for i in range(2):
                P.op("dve", lambda e, l=l, b=b, i=i: e.scalar_tensor_tensor(
                    out=ljunk[:], in0=par[:, b + 128 * i:b + 128 * i + 64], scalar=1.0, in1=par[:, b + 128 * i + 64:b + 128 * i + 128],
                    op0=ALU.mult, op1=ALU.mult, accum_out=lamt[:, 4 * l + i:4 * l + i + 1]),
                    reads=["par", "ljunk"], writes=["ljunk", "lamt"])
            P.op("act", lambda e, l=l: e.activation(out=lamt[:, 4 * l:4 * l + 2], in_=lamt[:, 4 * l:4 * l + 2], func=AF.Exp),
                 reads=["lamt"], writes=["lamt"])
            P.op("dve", lambda e, l=l: e.scalar_tensor_tensor(out=lamt[:, 4 * l + 2:4 * l + 3], in0=lamt[:, 4 * l:4 * l + 1],
                                                               scalar=float(lam_inits[l]), in1=lamt[:, 4 * l + 1:4 * l + 2],
                                                               op0=ALU.add, op1=ALU.subtract), reads=["lamt"], writes=["lamt"])
            P.op("dve", lambda e, l=l: e.tensor_scalar(out=lamt[:, 4 * l + 3:4 * l + 4], in0=lamt[:, 4 * l + 2:4 * l + 3],
                                                        scalar1=-1.0, scalar2=0.0, op0=ALU.mult, op1=ALU.add),
                 reads=["lamt"], writes=["lamt"])
        P.barrier()
        P.emit()

        def norm_group(xt, sq, rsd, ht, gcol0, keys, pbank, pkey):
            xk, sqk, rsk, hk = keys
            P.op("act", lambda e: e.activation(out=sq[:], in_=xt[:], func=AF.Square), reads=[xk], writes=[sqk])
            P.op("pe", [(lambda e, c=c: e.matmul(pbank[:], lhsT=ones_bf[:], rhs=sq[:, c, :], start=(c == 0), stop=(c == 7)))
                        for c in range(8)], reads=[sqk], writes=[pkey])
            P.op("act", lambda e: e.activation(out=rsd[:], in_=pbank[:], func=AF.Sqrt, scale=1.0 / D, bias=EPS),
                 reads=[pkey], writes=[rsk])
            P.op("dve", lambda e: e.reciprocal(out=rsd[:], in_=rsd[:]), reads=[rsk], writes=[rsk])
            for c in range(8):
                P.op("dve", lambda e, c=c: e.scalar_tensor_tensor(out=ht[:, c, :], in0=xt[:, c, :], scalar=par[:, gcol0 + c:gcol0 + c + 1],
                                                                   in1=rsd[:], op0=ALU.mult, op1=ALU.mult),
                     reads=[xk, rsk], writes=[hk])

        def wload(dst, src, key):
            P.dma("pool", lambda e: e.dma_start(out=dst, in_=src), "w_" + key, writes=[key])

        import os as _os
        PH = _os.environ.get("KPHASES", "A,B1,B2,C1,C2").split(",")
        for l in range(depth):
            x_src = xT_in if l == 0 else xs
            x_dst_final = yT if l == depth - 1 else xs
            lam_init = float(lam_inits[l])

            with ExitStack() as ph:
              if "A" in PH:
                wA = sbuf(ph, "wA", [128, 8, 2432], BF16)
                xT = [sbuf(ph, "xT%d" % i, [128, 8, 512], F32) for i in range(2)]
                sq = sbuf(ph, "sq", [128, 8, 512], BF16)
                rsd = sbuf(ph, "rsd", [128, 512], F32)
                hT = [sbuf(ph, "hT%d" % i, [128, 8, 512], BF16) for i in range(2)]
                sq2 = [sbuf(ph, "sq2_%d" % i, [128, 512], BF16) for i in range(4)]
                rs2 = [sbuf(ph, "rs2_%d" % i, [128, 512], F32) for i in range(2)]
                qn = [sbuf(ph, "qn%d" % i, [128, 512], BF16) for i in range(3)]
                vAs = [sbuf(ph, "vAs%d" % i, [128, 4, 130], BF16) for i in range(2)]
                vDs = [sbuf(ph, "vDs%d" % i, [128, 4, 516], BF16) for i in range(2)]
                wl = chunked(w_in[l])
                for (d0, s0, n, wk) in ((0, 0, 512, "wA0"), (512, 512, 64, "wA1"), (576, 512, 64, "wA1"), (640, 576, 64, "wA1"), (704, 576, 64, "wA1"),
                                        (768, 768, 1024, "wA2"), (1792, 640, 128, "wA3"), (1920, 1792, 512, "wA3")):
                    wload(wA[:, :, d0:d0 + n], wl[:, :, s0:s0 + n], wk)
                for i in range(2):
                    P.op("pool", lambda e, i=i: e.memset(vAs[i][:], 1.0), writes=["vAs%d" % i])
                    P.op("pool", lambda e, i=i: e.memset(vDs[i][:], 1.0), writes=["vDs%d" % i])
                rx, rh, rq2, rqn, rv = Ring("xT", 2), Ring("hT", 2), Ring("q2_", 4), Ring("qn", 3), Ring("vs", 2)
                rpq, rps, rr2 = Ring("pq", 4), Ring("pss", 1), Ring("rs2_", 2)
                xsrc_c = chunked(x_src)
                hslot = {}

                xslot = {}

                def load_x(tg):
                    xi = rx.next()
                    xslot[tg] = xi
                    P.dma("sp", lambda e: e.dma_start(out=xT[xi][:], in_=xsrc_c[:, :, tg * 512:(tg + 1) * 512]),
                          "xT%d" % xi, reads=["xs_%d" % tg], writes=[rx.key(xi)])

                def emit_norm(tg):
                    xi = xslot[tg]
                    hi = rh.next()
                    hslot[tg] = hi
                    norm_group(xT[xi], sq, rsd, hT[hi], PC_NM + 8 * l, (rx.key(xi), "sq", "rsd", rh.key(hi)), PB[6], "pb6")

                pst_ = {}

                def projA(tg, j):
                    hi = hslot[tg]
                    pi = rpq.next()
                    pq, pqk = PB[pi], rpq.key(pi)
                    P.op("pe", [(lambda e, c=c: e.matmul(pq, lhsT=wA[:, c, j * 128:(j + 1) * 128], rhs=hT[hi][:, c, :],
                                                         start=(c == 0), stop=(c == 7))) for c in range(8)],
                         reads=["wA0" if j < 4 else ("wA1" if j < 6 else "wA2"), rh.key(hi)], writes=[pqk])
                    qi = rq2.next()
                    P.op("act", lambda e: e.activation(out=sq2[qi][:], in_=pq, func=AF.Square), reads=[pqk], writes=[rq2.key(qi)])
                    pst_[(tg, j)] = (pq, pqk, qi)

                def restA(tg, j):
                    pq, pqk, qi = pst_.pop((tg, j))
                    si = rps.next()
                    pss, psk = PB[4 + si], rps.key(si)
                    P.op("pe", lambda e: e.matmul(pss, lhsT=blk_bf[:], rhs=sq2[qi][:], start=True, stop=True),
                         reads=[rq2.key(qi)], writes=[psk])
                    ri = rr2.next()
                    rk = rr2.key(ri)
                    P.op("act", lambda e: e.activation(out=rs2[ri][:], in_=pss, func=AF.Sqrt, scale=1.0 / 64, bias=EPS), reads=[psk], writes=[rk])
                    P.op("dve", lambda e: e.reciprocal(out=rs2[ri][:], in_=rs2[ri][:]), reads=[rk], writes=[rk])
                    if j < 4:
                        gc, dst, r0 = PC_QS + 2 * l, QaT, j * 128
                    elif j < 6:
                        gc, dst, r0 = PC_QS + 2 * l + 1, KaT, (j - 4) * 128
                    elif j < 10:
                        gc, dst, r0 = PC_QD + 2 * l, QdT, (j - 6) * 128
                    else:
                        gc, dst, r0 = PC_QD + 2 * l + 1, KdT, (j - 10) * 128
                    ni = rqn.next()
                    P.op("dve", lambda e: e.scalar_tensor_tensor(out=qn[ni][:], in0=pq, scalar=par[:, gc:gc + 1], in1=rs2[ri][:],
                                                                  op0=ALU.mult, op1=ALU.mult), reads=[pqk, rk], writes=[rqn.key(ni)])
                    P.dma("sp", lambda e: e.dma_start(out=dst[r0:r0 + 128, tg * 512:(tg + 1) * 512], in_=qn[ni][:]),
                          "qn%d" % ni, reads=[rqn.key(ni)], writes=["qk_%d" % tg])

                def vblock(tg, hi, hk, vi, tb):
                    P.op("pe", [(lambda e, c=c: e.matmul(PB[6][:, 0:128], lhsT=hT[hi][:, c, tb * 128:(tb + 1) * 128],
                                                         rhs=wA[:, c, 1792:1920], start=(c == 0), stop=(c == 7))) for c in range(8)],
                         reads=["wA3", hk], writes=["pb6"])
                    P.op("act", lambda e: e.activation(
                        out=vAs[vi][:, tb, :].rearrange("p (k d) -> p k d", k=2)[:, :, 0:64],
                        in_=PB[6][:, 0:128].rearrange("p (k d) -> p k d", k=2), func=AF.Copy),
                        reads=["pb6"], writes=["vAs%d" % vi])
                    P.op("pe", [(lambda e, c=c: e.matmul(PB[5], lhsT=hT[hi][:, c, tb * 128:(tb + 1) * 128],
                                                         rhs=wA[:, c, 1920:2432], start=(c == 0), stop=(c == 7))) for c in range(8)],
                         reads=["wA3", hk], writes=["pb5"])
                    P.op("dve", lambda e: e.tensor_copy(
                        out=vDs[vi][:, tb, :].rearrange("p (h d) -> p h d", h=4)[:, :, 0:128],
                        in_=PB[5].rearrange("p (h d) -> p h d", h=4)), reads=["pb5"], writes=["vDs%d" % vi])

                load_x(0)
                emit_norm(0)
                for tg in range(NG):
                    hi = hslot[tg]
                    hk = rh.key(hi)
                    if tg + 1 < NG:
                        load_x(tg + 1)
                    vi = rv.next()
                    projA(tg, 0)
                    for j in range(14):
                        if j + 1 < 14:
                            projA(tg, j + 1)
                        restA(tg, j)
                        if j in (1, 4, 7, 11):
                            vblock(tg, hi, hk, vi, (1, 4, 7, 11).index(j))
                        if j == 9 and tg + 1 < NG:
                            emit_norm(tg + 1)
                    P.dma("sp", lambda e, vi=vi, tg=tg: e.dma_start(out=Va[:, 4 * tg:4 * tg + 4, :], in_=vAs[vi][:]),
                          "vAs%d" % vi, reads=["vAs%d" % vi], writes=["va_%d" % tg])
                    for h in range(4):
                        P.dma("sp", lambda e, vi=vi, tg=tg, h=h: e.dma_start(out=Vd[h, :, 4 * tg:4 * tg + 4, :],
                                                                             in_=vDs[vi][:, :, h * 129:(h + 1) * 129]),
                              "vDs%d" % vi, reads=["vDs%d" % vi], writes=["vd_%d" % tg])
                P.barrier()
                P.emit()

            with ExitStack() as ph:
              if "B1" in PH:
                kA = [sbuf(ph, "kA%d" % i, [128, 2, 640], BF16) for i in range(2)]
                vA = [sbuf(ph, "vA%d" % i, [128, 5, 130], BF16) for i in range(2)]
                qA = [sbuf(ph, "qA%d" % i, [128, 4, 512], BF16) for i in range(2)]
                tmp = [sbuf(ph, "tmp%d" % i, [128, 256], F32) for i in range(4)]
                Es = [sbuf(ph, "Es%d" % i, [128, 256], BF16) for i in range(4)]
                den = [sbuf(ph, "den%d" % i, [128, 1], F32) for i in range(3)]
                oAs = [sbuf(ph, "oAs%d" % i, [128, 128], BF16) for i in range(2)]
                oaT = [sbuf(ph, "oaT%d" % i, [128, 4, 512], BF16) for i in range(2)]
                rin, rt, rE, rden, roA, roT = Ring("swin", 2), Ring("tmp", 4), Ring("Es", 4), Ring("den", 3), Ring("oAs", 2), Ring("oaT", 2)
                rS, rO = Ring("pS", 4), Ring("pO", 2)
                pSb = [PB[0], PB[1], PB[3], PB[5]]
                pOb = [PB[2], PB[4]]
                KaT_c = KaT.rearrange("(k p) t -> p k t", p=128)
                QaT_c = chunked(QaT)
                OaT_c = chunked(OaT)
                inslot = {}

                def load_in(tg):
                    ii = rin.next()
                    inslot[tg] = ii
                    ik = rin.key(ii)
                    if tg == 0:
                        P.op("pool", lambda e: e.memset(kA[ii][:, :, 0:128], 0.0), writes=[ik])
                        P.op("pool", lambda e: e.memset(vA[ii][:, 0, :], 0.0), reads=[ik], writes=[ik])
                        P.dma("sp", lambda e: e.dma_start(out=kA[ii][:, :, 128:640], in_=KaT_c[:, :, 0:512]), "swin%d" % ii, writes=[ik])
                        P.dma("sp", lambda e: e.dma_start(out=vA[ii][:, 1:5, :], in_=Va[:, 0:4, :]), "swin%d" % ii, writes=[ik])
                    else:
                        P.dma("sp", lambda e: e.dma_start(out=kA[ii][:], in_=KaT_c[:, :, tg * 512 - 128:tg * 512 + 512]), "swin%d" % ii, writes=[ik])
                        P.dma("sp", lambda e: e.dma_start(out=vA[ii][:], in_=Va[:, 4 * tg - 1:4 * tg + 4, :]), "swin%d" % ii, writes=[ik])
                    P.dma("sp", lambda e: e.dma_start(out=qA[ii][:], in_=QaT_c[:, :, tg * 512:(tg + 1) * 512]), "swin%d" % ii, writes=[ik])

                sst = {}

                def scB(tg, it):
                    qc, n, hh = it
                    ii = inslot[tg]
                    ik = rin.key(ii)
                    kv = qc // 2
                    h = 2 * qc + hh
                    pr = slice(64 * hh, 64 * hh + 64)
                    si = rS.next()
                    pS = pSb[si][:, 0:256]
                    psk = rS.key(si)
                    P.op("pe", [(lambda e, kb=kb: e.matmul(pS[:, kb * 128:(kb + 1) * 128], lhsT=kA[ii][pr, kv, (n + kb) * 128:(n + kb + 1) * 128],
                                                           rhs=qA[ii][pr, qc, n * 128:(n + 1) * 128], start=True, stop=True)) for kb in range(2)],
                         reads=[ik], writes=[psk])
                    ti = rt.next()
                    P.op("dve", lambda e: e.scalar_tensor_tensor(out=tmp[ti][:], in0=pS, scalar=0.125, in1=swab[:, h, :, :].rearrange("p a b -> p (a b)"),
                                                                  op0=ALU.mult, op1=ALU.add), reads=[psk], writes=[rt.key(ti)])
                    ei = rE.next()
                    P.op("act", lambda e: e.activation(out=Es[ei][:], in_=tmp[ti][:], func=AF.Exp), reads=[rt.key(ti)], writes=[rE.key(ei)])
                    sst[(tg, it)] = ei

                def pvB(tg, it, oi, oti):
                    qc, n, hh = it
                    ii = inslot[tg]
                    ik = rin.key(ii)
                    kv = qc // 2
                    h = 2 * qc + hh
                    ei = sst.pop((tg, it))
                    pi = rO.next()
                    pO = pOb[pi][:, 0:65]
                    pok = rO.key(pi)
                    P.op("pe", [(lambda e, kb=kb: e.matmul(pO, lhsT=Es[ei][:, kb * 128:(kb + 1) * 128], rhs=vA[ii][:, n + kb, kv * 65:kv * 65 + 65],
                                                           start=(kb == 0), stop=(kb == 1))) for kb in range(2)], reads=[rE.key(ei), ik], writes=[pok])
                    di = rden.next()
                    dk = rden.key(di)
                    P.op("dve", lambda e: e.tensor_scalar(out=den[di][:], in0=pO[:, 64:65], scalar1=esink[:, 8 * l + h:8 * l + h + 1], scalar2=None, op0=ALU.add),
                         reads=[pok], writes=[dk])
                    P.op("dve", lambda e: e.reciprocal(out=den[di][:], in_=den[di][:]), reads=[dk], writes=[dk])
                    P.op("dve", lambda e: e.tensor_scalar(out=oAs[oi][:, 64 * hh:64 * hh + 64], in0=pO[:, 0:64], scalar1=den[di][:, 0:1], scalar2=None, op0=ALU.mult),
                         reads=[pok, dk], writes=[roA.key(oi)])
                    if hh == 1:
                        pendT.append((oi, oti, qc, n))

                pendT = []

                def flushT():
                    while pendT:
                        oi, oti, qc, n = pendT.pop(0)
                        pT = PT[:, 0:128]
                        P.op("pe", lambda e, oi=oi: e.transpose(pT, oAs[oi][:], id_bf[:]), reads=[roA.key(oi)], writes=["pTb1"])
                        P.op("act", lambda e, oti=oti, qc=qc, n=n: e.activation(out=oaT[oti][:, qc, n * 128:(n + 1) * 128], in_=pT, func=AF.Copy),
                             reads=["pTb1"], writes=[roT.key(oti)])

                load_in(0)
                its = [(qc, n, hh) for qc in range(4) for n in range(4) for hh in range(2)]
                for tg in range(NG):
                    if tg + 1 < NG:
                        load_in(tg + 1)
                    oti = roT.next()
                    scB(tg, its[0])
                    scB(tg, its[1])
                    scB(tg, its[2])
                    oi = 0
                    for k, it in enumerate(its):
                        if k + 3 < len(its):
                            scB(tg, its[k + 3])
                        if it[2] == 0:
                            oi = roA.next()
                        had = len(pendT) > 0
                        pvB(tg, it, oi, oti)
                        if had:
                            flushT()
                    flushT()
                    P.dma("sp", lambda e, oti=oti, tg=tg: e.dma_start(out=OaT_c[:, :, tg * 512:(tg + 1) * 512], in_=oaT[oti][:]),
                          "oaT%d" % oti, reads=[roT.key(oti)], writes=["oa_%d" % tg])
                P.barrier()
                P.emit()

            with ExitStack() as ph:
              if "B2" in PH:
                Kh = [sbuf(ph, "Kh%d" % i, [128, S], BF16) for i in range(2)]
                Vh = [sbuf(ph, "Vh%d" % i, [128, NB, 129], BF16) for i in range(2)]
                qD = [sbuf(ph, "qD%d" % i, [128, 2, 256], BF16) for i in range(3)]
                Ed = [sbuf(ph, "Ed%d" % i, [128, 2, 256], BF16) for i in range(3)]
                rz = [sbuf(ph, "rz%d" % i, [128, 4], F32) for i in range(2)]
                o0 = [sbuf(ph, "o0_%d" % i, [128, 128], F32) for i in range(2)]
                oo = [sbuf(ph, "oo_%d" % i, [128, 128], F32) for i in range(2)]
                jk = sbuf(ph, "jk", [128, 128], F32)
                ob = [sbuf(ph, "ob_%d" % i, [128, 128], BF16) for i in range(2)]
                obT = [sbuf(ph, "obT%d" % i, [128, 256], BF16) for i in range(2)]
                accS = [[sbuf(ph, "accS%d_%d" % (i, k), [128, 129], F32) for k in range(4)] for i in range(2)]
                rKV, rq, rEd, rfin, robT = Ring("KV", 2), Ring("qD", 3), Ring("Ed", 3), Ring("fin", 2), Ring("obT", 2)
                rST, rT, rAS = Ring("pST", 2), Ring("pTd", 4), Ring("accS", 2)
                acck = ["acc%d" % i for i in range(4)]
                kvslot, qslot, est = {}, {}, {}

                def load_kv(h):
                    ki = rKV.next()
                    kvslot[h] = ki
                    kk = rKV.key(ki)
                    P.dma("sp", lambda e: e.dma_start(out=Kh[ki][:], in_=KdT[h * 128:(h + 1) * 128, :]), "KV%d" % ki, writes=[kk])
                    P.dma("sp", lambda e: e.dma_start(out=Vh[ki][:], in_=Vd[h]), "KV%d" % ki, writes=[kk])

                for i in range(3):
                    P.op("pool", lambda e, i=i: e.memset(qD[i][:], 0.0), writes=["qD%d" % i])

                def load_q(h, g2):
                    qi = rq.next()
                    qslot[(h, g2)] = qi
                    for c in range(2):
                        P.dma("sp", lambda e, c=c: e.dma_start(out=qD[qi][64 * c:64 * c + 64, c, :],
                                                             in_=QdT[h * 128 + 64 * c:h * 128 + 64 * c + 64, g2 * 256:(g2 + 1) * 256]),
                              "qD%d" % qi, writes=[rq.key(qi)])

                def scores(h, g2, m):
                    ki, qi = kvslot[h], qslot[(h, g2)]
                    kk, qk = rKV.key(ki), rq.key(qi)
                    c0 = 0 if m <= 2 * g2 else 128
                    dd = 2 * g2 - m
                    sti = rST.next()
                    stk = rST.key(sti)
                    pst = PB[4 + sti].rearrange("p (c q) -> p c q", c=2)
                    if c0 == 0:
                        P.op("pe", lambda e: e.matmul(PB[4 + sti], lhsT=Kh[ki][:, m * 128:(m + 1) * 128],
                                                      rhs=qD[qi][:].rearrange("p c q -> p (c q)"), start=True, stop=True),
                             reads=[kk, qk], writes=[stk])
                    else:
                        P.op("pe", [(lambda e, c=c: e.matmul(pst[:, c, c0:256], lhsT=Kh[ki][:, m * 128:(m + 1) * 128],
                                                             rhs=qD[qi][:, c, c0:256], start=True, stop=True)) for c in range(2)],
                             reads=[kk, qk], writes=[stk])
                    ei = rEd.next()
                    ek = rEd.key(ei)
                    P.op("act", lambda e: e.activation(
                        out=Ed[ei][:, :, c0:256], in_=pst[:, :, c0:256], func=AF.Exp,
                        scale=0.125, bias=dtab[:, h, dd + 1:dd + 2]), reads=[stk], writes=[ek])
                    if m >= 2 * g2:
                        for c in range(2):
                            P.op("pool", lambda e, c=c: e.tensor_tensor(
                                out=Ed[ei][:, c, c0:c0 + 128], in0=Ed[ei][:, c, c0:c0 + 128], in1=tri_bf[:], op=ALU.mult),
                                reads=[ek], writes=[ek])
                    est[(h, g2, m)] = (ei, c0 // 128)

                def pv(h, g2, m):
                    ei, nq0 = est.pop((h, g2, m))
                    ki = kvslot[h]
                    fns, wr = [], []
                    for c in range(2):
                        for nn in range(nq0, 2):
                            fns.append(lambda e, c=c, nn=nn: e.matmul(
                                PB[2 * c + nn][:, 0:129], lhsT=Ed[ei][:, c, nn * 128:(nn + 1) * 128], rhs=Vh[ki][:, m, :],
                                start=(m == 0), stop=(m == 2 * g2 + nn)))
                            wr.append(acck[2 * c + nn])
                    P.op("pe", fns, reads=[rEd.key(ei), rKV.key(ki)], writes=wr)

                pending = []

                def finalize(h, g2):
                    finalize2()
                    si = rAS.next()
                    sk = rAS.key(si)
                    A = accS[si]
                    for k in range(4):
                        P.op("dve", lambda e, k=k: e.tensor_copy(out=A[k][:], in_=PB[k][:, 0:129]), reads=[acck[k]], writes=[sk + "_%d" % k])
                    oi = robT.next()
                    obk = robT.key(oi)
                    fis = []
                    for nn in range(2):
                        fi = rfin.next()
                        fk = rfin.key(fi)
                        a0, a1 = A[nn], A[2 + nn]
                        k0, k1 = sk + "_%d" % nn, sk + "_%d" % (2 + nn)
                        P.op("dve", lambda e, fi=fi, a0=a0: e.reciprocal(out=rz[fi][:, 0:1], in_=a0[:, 128:129]), reads=[k0], writes=[fk])
                        P.op("dve", lambda e, fi=fi, a1=a1: e.reciprocal(out=rz[fi][:, 1:2], in_=a1[:, 128:129]), reads=[k1, fk], writes=[fk])
                        P.op("dve", lambda e, fi=fi: e.tensor_tensor(out=rz[fi][:, 2:3], in0=rz[fi][:, 1:2], in1=lamt[:, 4 * l + 3:4 * l + 4], op=ALU.mult),
                             reads=[fk], writes=[fk])
                        P.op("dve", lambda e, fi=fi, a0=a0: e.tensor_scalar(out=o0[fi][:], in0=a0[:, 0:128], scalar1=rz[fi][:, 0:1], scalar2=None, op0=ALU.mult),
                             reads=[k0, fk], writes=[fk])
                        P.op("dve", lambda e, fi=fi, a1=a1: e.scalar_tensor_tensor(out=oo[fi][:], in0=a1[:, 0:128], scalar=rz[fi][:, 2:3], in1=o0[fi][:],
                                                                                   op0=ALU.mult, op1=ALU.add), reads=[k1, fk], writes=[fk])
                        P.op("dve", lambda e, fi=fi: e.scalar_tensor_tensor(out=jk[:], in0=oo[fi][:], scalar=1.0, in1=oo[fi][:], op0=ALU.mult, op1=ALU.mult,
                                                                            accum_out=rz[fi][:, 3:4]), reads=[fk, "jk"], writes=[fk, "jk"])
                        P.op("dve", lambda e, fi=fi: e.tensor_scalar(out=rz[fi][:, 3:4], in0=rz[fi][:, 3:4], scalar1=1.0 / 128, scalar2=EPS, op0=ALU.mult, op1=ALU.add),
                             reads=[fk], writes=[fk])
                        P.op("pool", lambda e, fi=fi: e.tensor_tensor(out=rz[fi][:, 3:4], in0=rz[fi][:, 3:4], in1=nhalf[:, 0:1], op=ALU.pow),
                             reads=[fk], writes=[fk])
                        P.op("dve", lambda e, fi=fi: e.tensor_scalar(out=ob[fi][:], in0=oo[fi][:], scalar1=rz[fi][:, 3:4], scalar2=None, op0=ALU.mult),
                             reads=[fk], writes=[fk])
                        fis.append((fi, fk))
                    pending.append((h, g2, oi, obk, fis))

                def finalize2():
                    while pending:
                        h, g2, oi, obk, fis = pending.pop(0)
                        for nn, (fi, fk) in enumerate(fis):
                            ti2 = 0
                            pT = PT[:, ti2 * 128:(ti2 + 1) * 128]
                            P.op("pe", lambda e, pT=pT, fi=fi: e.transpose(pT, ob[fi][:], id_bf[:]), reads=[fk], writes=[rT.key(ti2)])
                            P.op("dve", lambda e, pT=pT, nn=nn, oi=oi: e.tensor_scalar(
                                out=obT[oi][:, nn * 128:(nn + 1) * 128], in0=pT, scalar1=par[:, PC_SL + l:PC_SL + l + 1], scalar2=1.0 - lam_init,
                                op0=ALU.mult, op1=ALU.mult), reads=[rT.key(ti2)], writes=[obk])
                        P.dma("sp", lambda e, h=h, g2=g2, oi=oi: e.dma_start(out=ObT[h * 128:(h + 1) * 128, g2 * 256:(g2 + 1) * 256], in_=obT[oi][:]),
                              "obT%d" % oi, reads=[obk], writes=["ob_%d_%d" % (h, g2)])

                items = [(h, g2, m) for h in range(4) for g2 in range(NG2) for m in range(2 * g2 + 2)]
                groups = [(h, g2) for h in range(4) for g2 in range(NG2)]
                gidx = {g: i for i, g in enumerate(groups)}

                def prep(i):
                    h, g2, m = items[i]
                    if m == 0:
                        gi = gidx[(h, g2)]
                        if gi == 0:
                            load_kv(0)
                            load_q(0, 0)
                        if g2 == 1 and h + 1 < 4:
                            load_kv(h + 1)
                        if gi + 1 < len(groups):
                            load_q(*groups[gi + 1])
                    scores(h, g2, m)

                prep(0)
                since = 0
                for i in range(len(items)):
                    if i + 1 < len(items):
                        prep(i + 1)
                    h, g2, m = items[i]
                    pv(h, g2, m)
                    since += 1
                    if pending and since >= 6:
                        finalize2()
                    if m == 2 * g2 + 1:
                        finalize(h, g2)
                        since = 0
                finalize2()
                P.barrier()
                P.emit()

            with ExitStack() as ph:
              if "C1" in PH:
                wG = sbuf(ph, "wG", [128, 8, 2048], BF16)
                wB = sbuf(ph, "wB", [128, 2, 4, 1024], BF16)
                wO = sbuf(ph, "wO", [128, 8, 1024], BF16)
                xT = [sbuf(ph, "xT%d" % i, [128, 8, 512], F32) for i in range(2)]
                sq = sbuf(ph, "sq", [128, 8, 512], BF16)
                rsd = sbuf(ph, "rsd", [128, 512], F32)
                hT = [sbuf(ph, "hT%d" % i, [128, 8, 512], BF16) for i in range(2)]
                gT = sbuf(ph, "gT", [128, 16, 512], BF16)
                oab = [sbuf(ph, "oab%d" % i, [128, 8, 512], BF16) for i in range(2)]
                m1 = [sbuf(ph, "m1_%d" % i, [128, 512], F32) for i in range(2)]
                m2 = [sbuf(ph, "m2_%d" % i, [128, 512], F32) for i in range(2)]
                mT = sbuf(ph, "mT", [128, 8, 512], BF16)
                wl = chunked(w_in[l])
                for q4 in range(4):
                    wload(wG[:, :, q4 * 512:(q4 + 1) * 512], wl[:, :, 2304 + q4 * 512:2304 + (q4 + 1) * 512], "wG%d" % q4)
                for n in range(2):
                    wload(wB[:, n, :, :], w_br[l, n].rearrange("(k p) d -> p k d", p=128), "wB")
                wol = chunked(w_o[l])
                for q2 in range(2):
                    wload(wO[:, :, q2 * 512:(q2 + 1) * 512], wol[:, :, q2 * 512:(q2 + 1) * 512], "wO")
                rx, roab, rm = Ring("xT", 2), Ring("oab", 2), Ring("m", 2)
                rpq = Ring("pq", 2)
                rpa = Ring("pa", 2)
                xsrc_c = chunked(x_src)
                xs_c = chunked(xs)
                OaT_c, ObT_c = chunked(OaT), chunked(ObT)
                c1slot = {}

                def pro_c1(tg):
                    xi = rx.next()
                    xk = rx.key(xi)
                    P.dma("sp", lambda e: e.dma_start(out=xT[xi][:], in_=xsrc_c[:, :, tg * 512:(tg + 1) * 512]),
                          "xT%d" % xi, reads=["xs_%d" % tg], writes=[xk])
                    ai = roab.next()
                    ak = roab.key(ai)
                    P.dma("sp", lambda e: e.dma_start(out=oab[ai][:, 0:4, :], in_=OaT_c[:, :, tg * 512:(tg + 1) * 512]), "oab%d" % ai, writes=[ak])
                    P.dma("sp", lambda e: e.dma_start(out=oab[ai][:, 4:8, :], in_=ObT_c[:, :, tg * 512:(tg + 1) * 512]), "oab%d" % ai, writes=[ak])
                    c1slot[tg] = (xi, ai)

                rhc = Ring("hTc", 2)
                hcs = {}

                def norm_c1(tg):
                    xi_, _ = c1slot[tg]
                    hi_ = rhc.next()
                    hcs[tg] = hi_
                    norm_group(xT[xi_], sq, rsd, hT[hi_], PC_NM + 8 * l, (rx.key(xi_), "sq", "rsd", rhc.key(hi_)), PB[6], "pb6")

                pro_c1(0)
                norm_c1(0)
                for tg in range(NG):
                    xi, ai = c1slot[tg]
                    xk, ak = rx.key(xi), roab.key(ai)
                    hi = hcs[tg]
                    hk = rhc.key(hi)
                    if tg + 1 < NG:
                        pro_c1(tg + 1)
                    for j in range(16):
                        if j == 10 and tg + 1 < NG:
                            norm_c1(tg + 1)
                        pi = rpq.next()
                        pq, pqk = PB[pi], rpq.key(pi)
                        P.op("pe", [(lambda e, c=c, j=j, pq=pq, hi=hi: e.matmul(pq[:], lhsT=wG[:, c, j * 128:(j + 1) * 128], rhs=hT[hi][:, c, :],
                                                                                  start=(c == 0), stop=(c == 7))) for c in range(8)],
                             reads=["wG%d" % (j // 4), hk], writes=[pqk])
                        P.op("act", lambda e, j=j, pq=pq: e.activation(out=gT[:, j, :], in_=pq[:], func=AF.Sigmoid,
                                                                       bias=par[:, PC_BG + 16 * l + j:PC_BG + 16 * l + j + 1]),
                             reads=[pqk], writes=["gT"])
                    for j in range(8):
                        pi = rpa.next()
                        pa, pb_ = PB[2 + 2 * pi], PB[3 + 2 * pi]
                        pak = rpa.key(pi)
                        P.op("pe", [(lambda e, kc=kc, j=j, pa=pa, ai=ai: e.matmul(pa[:], lhsT=wB[:, 0, kc, j * 128:(j + 1) * 128], rhs=oab[ai][:, kc, :],
                                                                                     start=(kc == 0), stop=(kc == 3))) for kc in range(4)] +
                                   [(lambda e, kc=kc, j=j, pb_=pb_, ai=ai: e.matmul(pb_[:], lhsT=wB[:, 1, kc, j * 128:(j + 1) * 128], rhs=oab[ai][:, 4 + kc, :],
                                                                                       start=(kc == 0), stop=(kc == 3))) for kc in range(4)],
                             reads=["wB", ak], writes=[pak])
                        mi = rm.next()
                        mk = rm.key(mi)
                        P.op("dve", lambda e, mi=mi, pa=pa, j=j: e.tensor_tensor(out=m1[mi][:], in0=pa[:], in1=gT[:, j, :], op=ALU.mult),
                             reads=[pak, "gT"], writes=[mk])
                        P.op("dve", lambda e, mi=mi, pb_=pb_, j=j: e.tensor_tensor(out=m2[mi][:], in0=pb_[:], in1=gT[:, 8 + j, :], op=ALU.mult),
                             reads=[pak, "gT", mk], writes=[mk])
                        P.op("pool", lambda e, mi=mi, j=j: e.tensor_tensor(out=mT[:, j, :], in0=m1[mi][:], in1=m2[mi][:], op=ALU.add),
                             reads=[mk], writes=["mT"])
                    for j in range(8):
                        pi = rpq.next()
                        pq, pqk = PB[pi], rpq.key(pi)
                        P.op("pe", [(lambda e, c=c, j=j, pq=pq: e.matmul(pq[:], lhsT=wO[:, c, j * 128:(j + 1) * 128], rhs=mT[:, c, :],
                                                                           start=(c == 0), stop=(c == 7))) for c in range(8)],
                             reads=["wO", "mT"], writes=[pqk])
                        P.op("dve", lambda e, j=j, pq=pq, xi=xi: e.tensor_tensor(out=xT[xi][:, j, :], in0=xT[xi][:, j, :], in1=pq[:], op=ALU.add),
                             reads=[pqk, xk], writes=[xk])
                    P.dma("sp", lambda e, xi=xi, tg=tg: e.dma_start(out=xs_c[:, :, tg * 512:(tg + 1) * 512], in_=xT[xi][:]),
                          "xo%d" % xi, reads=[xk], writes=["xs_%d" % tg])
                P.barrier()
                P.emit()

            for half in range(2):
                with ExitStack() as ph:
                  if "C2" in PH:
                    wI = sbuf(ph, "wI", [128, 8, 2, 1408], BF16)
                    wF = sbuf(ph, "wF", [128, 11, 1024], BF16)
                    xT = [sbuf(ph, "xT%d" % i, [128, 8, 512], F32) for i in range(2)]
                    sq = sbuf(ph, "sq", [128, 8, 512], BF16)
                    rsd = sbuf(ph, "rsd", [128, 512], F32)
                    h2 = [sbuf(ph, "h2_%d" % i, [128, 8, 512], BF16) for i in range(2)]
                    sil = [sbuf(ph, "sil%d" % i, [128, 512], F32) for i in range(2)]
                    aT = sbuf(ph, "aT", [128, 11, 512], BF16)
                    wil = chunked(w_fi[l])
                    for gu in range(2):
                        wload(wI[:, :, gu, :], wil[:, :, gu * HID + half * 1408:gu * HID + (half + 1) * 1408], "wI")
                    wfl = w_fo[l, half * 1408:(half + 1) * 1408, :].rearrange("(k p) d -> p k d", p=128)
                    for q2 in range(2):
                        wload(wF[:, :, q2 * 512:(q2 + 1) * 512], wfl[:, :, q2 * 512:(q2 + 1) * 512], "wF")
                    rx, rh2, rsl = Ring("xT", 2), Ring("h2_", 2), Ring("sil", 2)
                    rpg, rpo = Ring("pg", 2), Ring("po", 2)
                    xs_c = chunked(xs)
                    H2_c = chunked(H2T)
                    dst_c = chunked(x_dst_final if half == 1 else xs)
                    c2slot = {}

                    def pro_c2(tg):
                        xi = rx.next()
                        xk = rx.key(xi)
                        P.dma("sp", lambda e: e.dma_start(out=xT[xi][:], in_=xs_c[:, :, tg * 512:(tg + 1) * 512]),
                              "xT%d" % xi, reads=["xs_%d" % tg], writes=[xk])
                        hi = rh2.next()
                        hk = rh2.key(hi)
                        if half == 1:
                            P.dma("sp", lambda e: e.dma_start(out=h2[hi][:], in_=H2_c[:, :, tg * 512:(tg + 1) * 512]),
                                  "h2i%d" % hi, reads=["h2_%d" % tg], writes=[hk])
                        c2slot[tg] = (xi, hi)

                    def norm_c2(tg):
                        xi_, hi_ = c2slot[tg]
                        norm_group(xT[xi_], sq, rsd, h2[hi_], PC_NF + 8 * l, (rx.key(xi_), "sq", "rsd", rh2.key(hi_)), PB[6], "pb6")
                        P.dma("sp", lambda e: e.dma_start(out=H2_c[:, :, tg * 512:(tg + 1) * 512], in_=h2[hi_][:]),
                              "h2o%d" % hi_, reads=[rh2.key(hi_)], writes=["h2_%d" % tg])

                    pro_c2(0)
                    for tg in range(NG):
                        xi, hi = c2slot[tg]
                        xk, hk = rx.key(xi), rh2.key(hi)
                        if tg + 1 < NG:
                            pro_c2(tg + 1)
                        if half == 0 and tg == 0:
                            norm_c2(0)
                        for jj in range(11):
                            if half == 0 and jj == 6 and tg + 1 < NG:
                                norm_c2(tg + 1)
                            pi = rpg.next()
                            pg, pu = PB[2 * pi], PB[2 * pi + 1]
                            pgk = rpg.key(pi)
                            P.op("pe", [(lambda e, c=c, jj=jj, pg=pg, hi=hi: e.matmul(pg[:], lhsT=wI[:, c, 0, jj * 128:(jj + 1) * 128], rhs=h2[hi][:, c, :],
                                                                                         start=(c == 0), stop=(c == 7))) for c in range(8)] +
                                       [(lambda e, c=c, jj=jj, pu=pu, hi=hi: e.matmul(pu[:], lhsT=wI[:, c, 1, jj * 128:(jj + 1) * 128], rhs=h2[hi][:, c, :],
                                                                                         start=(c == 0), stop=(c == 7))) for c in range(8)],
                                 reads=["wI", hk], writes=[pgk])
                            si = rsl.next()
                            P.op("act", lambda e, si=si, pg=pg: e.activation(out=sil[si][:], in_=pg[:], func=AF.Silu), reads=[pgk], writes=[rsl.key(si)])
                            P.op("dve", lambda e, si=si, pu=pu, jj=jj: e.tensor_tensor(out=aT[:, jj, :], in0=sil[si][:], in1=pu[:], op=ALU.mult),
                                 reads=[pgk, rsl.key(si)], writes=["aT"])
                        for j in range(8):
                            pi = rpo.next()
                            po, pok = PB[4 + pi], rpo.key(pi)
                            P.op("pe", [(lambda e, jj=jj, j=j, po=po: e.matmul(po[:], lhsT=wF[:, jj, j * 128:(j + 1) * 128], rhs=aT[:, jj, :],
                                                                                 start=(jj == 0), stop=(jj == 10))) for jj in range(11)],
                                 reads=["wF", "aT"], writes=[pok])
                            P.op("dve", lambda e, j=j, po=po, xi=xi: e.tensor_tensor(out=xT[xi][:, j, :], in0=xT[xi][:, j, :], in1=po[:], op=ALU.add),
                                 reads=[pok, xk], writes=[xk])
                        P.dma("sp", lambda e, xi=xi, tg=tg: e.dma_start(out=dst_c[:, :, tg * 512:(tg + 1) * 512], in_=xT[xi][:]),
                              "xo%d" % xi, reads=[xk], writes=["xs_%d" % tg])
                    P.barrier()
                    P.emit()
        build_nc.n_ins = P.n_ins
    return nc


def pack_params(b_gate, norm_mix, norm_ffn, qk_norm_swa, qk_norm_diff, attn_sinks, diff_lambda, diff_subln):
    depth = b_gate.shape[0]
    par = np.zeros((128, NPAR), np.float32)
    f = lambda a: np.asarray(a, np.float32)
    par[:, PC_BG:PC_BG + 16 * depth] = f(b_gate).reshape(depth, 16, 128).transpose(2, 0, 1).reshape(128, 16 * depth)
    par[:, PC_NM:PC_NM + 8 * depth] = f(norm_mix).reshape(depth, 8, 128).transpose(2, 0, 1).reshape(128, 8 * depth)
    par[:, PC_NF:PC_NF + 8 * depth] = f(norm_ffn).reshape(depth, 8, 128).transpose(2, 0, 1).reshape(128, 8 * depth)
    qs = f(qk_norm_swa).reshape(depth * 2, 64).T
    par[:, PC_QS:PC_QS + 2 * depth] = np.concatenate([qs, qs], axis=0)
    qd = f(qk_norm_diff).reshape(depth * 2, 64).T
    par[:, PC_QD:PC_QD + 2 * depth] = np.concatenate([qd, qd], axis=0)
    par[:, PC_SK:PC_SK + 8 * depth] = np.broadcast_to(f(attn_sinks).reshape(1, 8 * depth), (128, 8 * depth))
    par[:, PC_SL:PC_SL + depth] = f(diff_subln).T
    par[:, PC_LM:PC_LM + 256 * depth] = np.broadcast_to(f(diff_lambda).reshape(1, 256 * depth), (128, 256 * depth))
    return par


def run_model(x, w_in, b_gate, w_branch, w_o, norm_mix, norm_ffn, qk_norm_swa, qk_norm_diff,
              attn_sinks, diff_lambda, diff_subln, w_ffn_in, w_ffn_out, runner=None):
    x = np.asarray(x, np.float32)
    B, S, _ = x.shape
    depth = np.asarray(w_in).shape[0]
    lam_inits = [0.8 - 0.6 * math.exp(-0.3 * l) for l in range(depth)]
    nc = build_nc(S, depth, lam_inits)
    par = pack_params(b_gate, norm_mix, norm_ffn, qk_norm_swa, qk_norm_diff, attn_sinks, diff_lambda, diff_subln)
    ws = {"w_in": np.ascontiguousarray(w_in, np.float32), "w_branch": np.ascontiguousarray(w_branch, np.float32),
          "w_o": np.ascontiguousarray(w_o, np.float32), "w_ffn_in": np.ascontiguousarray(w_ffn_in, np.float32),
          "w_ffn_out": np.ascontiguousarray(w_ffn_out, np.float32), "par": par}
    n_cores = 8
    in_maps = []
    for c in range(n_cores):
        b = (c * B) // n_cores
        m = {"xT": np.ascontiguousarray(x[b].T)}
        m.update(ws)
        in_maps.append(m)
    if runner is None:
        res = run_bass_kernel_spmd(nc, in_maps, core_ids=list(range(n_cores))).results
    else:
        res = runner(nc, in_maps)
    out = np.empty((B, S, D), np.float32)
    per = n_cores // B
    for b in range(B):
        out[b] = res[b * per]["yT"].T
    return out


def kernel(**inputs):
    return run_model(**inputs)
```
